# Optimizing a Trainium2 kernel written in Bass

```python
import jax
import jax.numpy as jnp
from jax import lax
import numpy as np

D_MODEL = 4096
BATCH = 8
SEQ = 2048
DEPTH = 2

GRID_W = 64
CTX_LEN = 256
HEAD_DIM = 128
ATTN_WIDTH = D_MODEL // 2
N_Q_HEADS = ATTN_WIDTH // HEAD_DIM
N_KV_HEADS = N_Q_HEADS // 4
KV_GROUP = N_Q_HEADS // N_KV_HEADS
KV_WIDTH = N_KV_HEADS * HEAD_DIM
Q_BLOCK = 128
ROPE_THETA = 10000.0
ROPE_AXIS_DIM = HEAD_DIM // 2
HGRN_WIDTH = D_MODEL - ATTN_WIDTH
HGRN_HEADS = HGRN_WIDTH // HEAD_DIM
HGRN_EXPAND = HEAD_DIM
HGRN_KEY_WIDTH = HGRN_HEADS * HGRN_EXPAND
HGRN_CHUNK = 32
AB_SPLIT_SIZES = (ATTN_WIDTH, KV_WIDTH, KV_WIDTH, HGRN_KEY_WIDTH, HGRN_KEY_WIDTH, HGRN_KEY_WIDTH, HGRN_WIDTH, HGRN_WIDTH)
AB_IN = ATTN_WIDTH + 2 * KV_WIDTH + 3 * HGRN_KEY_WIDTH + 2 * HGRN_WIDTH
AB_OUT = ATTN_WIDTH + HGRN_WIDTH
CONV_WIDTH = 3
D_FF = 2 * D_MODEL
N_MOD = 9
N_EVEN = (DEPTH + 1) // 2
N_ODD = DEPTH // 2
EPS = 1e-6

kernel_name = 'hybrid_diffusion_gqa_hgrn2_shortconv_macaron'


def rms_norm(x, gain):
    xf = x.astype(jnp.float32)
    y = xf * lax.rsqrt(jnp.mean(xf * xf, axis=-1, keepdims=True) + EPS)
    return (y * gain.astype(jnp.float32)).astype(x.dtype)


def adaln_in(u, gain, shift, scale):
    return rms_norm(u, gain) * (1.0 + scale) + shift


def swiglu(h, w1, w2):
    gate, up = jnp.split(h @ w1, 2, axis=-1)
    return (jax.nn.silu(gate) * up) @ w2


def axial_rope_tables(n_tokens):
    rows = n_tokens // GRID_W
    row_ids = jnp.repeat(jnp.arange(rows, dtype=jnp.float32), GRID_W)
    col_ids = jnp.tile(jnp.arange(GRID_W, dtype=jnp.float32), rows)
    inv_freq = ROPE_THETA ** (-jnp.arange(0, ROPE_AXIS_DIM, 2, dtype=jnp.float32) / ROPE_AXIS_DIM)
    ang = jnp.stack([row_ids[:, None] * inv_freq, col_ids[:, None] * inv_freq], axis=1)
    return jnp.cos(ang), jnp.sin(ang)


def apply_axial_rope(x, cos, sin):
    b, t, hh, d = x.shape
    xr = x.reshape(b, t, hh, 2, 2, ROPE_AXIS_DIM // 2)
    x1 = xr[..., 0, :]
    x2 = xr[..., 1, :]
    cs = cos[None, :, None].astype(x.dtype)
    sn = sin[None, :, None].astype(x.dtype)
    out = jnp.stack([x1 * cs - x2 * sn, x2 * cs + x1 * sn], axis=-2)
    return out.reshape(b, t, hh, d)


def attend(q, k, v):
    b, t = q.shape[:2]
    n_blk = t // Q_BLOCK
    qb = q.reshape(b, n_blk, Q_BLOCK, N_KV_HEADS, KV_GROUP, HEAD_DIM).transpose(1, 0, 2, 3, 4, 5)
    scale = HEAD_DIM ** -0.5

    def one_block(q_blk):
        s = jnp.einsum('bqkgd,bskd->bkgqs', q_blk, k).astype(jnp.float32) * scale
        p = jax.nn.softmax(s, axis=-1).astype(v.dtype)
        return jnp.einsum('bkgqs,bskd->bqkgd', p, v)

    o = lax.map(one_block, qb)
    return o.transpose(1, 0, 2, 3, 4, 5).reshape(b, t, N_Q_HEADS * HEAD_DIM)


def hgrn_heads(a):
    return a.astype(jnp.float32).reshape(a.shape[0], a.shape[1], HGRN_HEADS, -1)


def hgrn_key_decay(z, lb):
    zf = z.astype(jnp.float32)
    log_f = jnp.log(lb + (1.0 - lb) * jax.nn.sigmoid(zf))
    k = (1.0 - lb) * jax.nn.sigmoid(-zf)
    return hgrn_heads(k), hgrn_heads(log_f)


def gla_chunk_scan(q, k, v, log_f, s0):
    b, t, hh, dk = q.shape
    dv = v.shape[-1]
    L = HGRN_CHUNK
    n_c = t // L

    def to_chunks(a):
        return a.reshape(b, n_c, L, hh, a.shape[-1]).transpose(1, 0, 3, 2, 4)

    mask = jnp.tril(jnp.ones((L, L), dtype=bool))

    def step(state, inp):
        qc, kc, vc, gc = inp
        cum = jnp.cumsum(gc, axis=-2)
        ref = cum[..., L // 2:L // 2 + 1, :]
        end = cum[..., L - 1:, :]
        attn = jnp.einsum('bhld,bhmd->bhlm', qc * jnp.exp(cum - ref), kc * jnp.exp(ref - cum))
        attn = jnp.where(mask, attn, 0.0)
        o = jnp.einsum('bhlm,bhme->bhle', attn, vc) + jnp.einsum('bhld,bhde->bhle', qc * jnp.exp(cum), state)
        new_state = jnp.exp(end[..., 0, :])[..., None] * state + jnp.einsum('bhld,bhle->bhde', kc * jnp.exp(end - cum), vc)
        return new_state, o

    s_fin, o = lax.scan(step, s0, (to_chunks(q), to_chunks(k), to_chunks(v), to_chunks(log_f)))
    return o.transpose(1, 0, 3, 2, 4).reshape(b, t, hh, dv), s_fin


def gla_final_state(k, v, log_f):
    cum = jnp.cumsum(log_f, axis=1)
    return jnp.einsum('bthd,bthe->bhde', k * jnp.exp(cum[:, -1:] - cum), v)


def flip_seq(a, direction):
    return jnp.flip(a, axis=1) if direction == 1 else a


def hgrn_bidirectional(q, v, z_dirs, qc, vc, zc_dirs, lb, need_ctx_out):
    b = q.shape[0]
    s_zero = jnp.zeros((b, HGRN_HEADS, HGRN_EXPAND, v.shape[-1]), jnp.float32)
    o_lat = []
    o_ctx = []
    for d in range(2):
        k, lf = hgrn_key_decay(z_dirs[d], lb[d])
        kc, lfc = hgrn_key_decay(zc_dirs[d], lb[d])
        if need_ctx_out:
            oc, s_c = gla_chunk_scan(flip_seq(qc, d), flip_seq(kc, d), flip_seq(vc, d), flip_seq(lfc, d), s_zero)
            o_ctx.append(flip_seq(oc, d))
        else:
            s_c = gla_final_state(flip_seq(kc, d), flip_seq(vc, d), flip_seq(lfc, d))
        ol, _ = gla_chunk_scan(flip_seq(q, d), flip_seq(k, d), flip_seq(v, d), flip_seq(lf, d), s_c)
        o_lat.append(flip_seq(ol, d))
    o_ctx_sum = (o_ctx[0] + o_ctx[1]) if need_ctx_out else None
    return o_lat[0] + o_lat[1], o_ctx_sum


def hgrn_gate_out(o, g, gain):
    b, t = o.shape[:2]
    return rms_norm(o, gain).reshape(b, t, -1).astype(g.dtype) * jax.nn.silu(g)


def mixer_ab(h, hc, w_in, w_out, q_gain, k_gain, lb, out_gain, cos, sin, need_ctx_out):
    split_idx = np.cumsum(AB_SPLIT_SIZES)[:-1].tolist()
    qa, ka, va, qb, zf, zb, ib, gb = jnp.split(h @ w_in, split_idx, axis=-1)
    qa_c, ka_c, va_c, qb_c, zf_c, zb_c, ib_c, gb_c = jnp.split(hc @ w_in, split_idx, axis=-1)
    b, t = h.shape[:2]
    tc = hc.shape[1]
    q = apply_axial_rope(rms_norm(qa.reshape(b, t, N_Q_HEADS, HEAD_DIM), q_gain), cos, sin)
    k = apply_axial_rope(rms_norm(ka.reshape(b, t, N_KV_HEADS, HEAD_DIM), k_gain), cos, sin)
    v = va.reshape(b, t, N_KV_HEADS, HEAD_DIM)
    k_c = rms_norm(ka_c.reshape(b, tc, N_KV_HEADS, HEAD_DIM), k_gain)
    v_c = va_c.reshape(b, tc, N_KV_HEADS, HEAD_DIM)
    o_attn = attend(q, jnp.concatenate([k, k_c], axis=1), jnp.concatenate([v, v_c], axis=1))
    q_h = hgrn_heads(jax.nn.silu(qb))
    v_h = hgrn_heads(ib)
    v_hc = hgrn_heads(ib_c)
    q_hc = hgrn_heads(jax.nn.silu(qb_c)) if need_ctx_out else None
    o_rec, o_rec_c = hgrn_bidirectional(q_h, v_h, (zf, zb), q_hc, v_hc, (zf_c, zb_c), lb, need_ctx_out)
    y = jnp.concatenate([o_attn, hgrn_gate_out(o_rec, gb, out_gain)], axis=-1) @ w_out
    if not need_ctx_out:
        return y, None
    q_c = rms_norm(qa_c.reshape(b, tc, N_Q_HEADS, HEAD_DIM), q_gain)
    o_attn_c = attend(q_c, k_c, v_c)
    y_c = jnp.concatenate([o_attn_c, hgrn_gate_out(o_rec_c, gb_c, out_gain)], axis=-1) @ w_out
    return y, y_c


def mixer_conv(h, w_in, conv_w, w_out):
    b_gate, c_gate, u = jnp.split(h @ w_in, 3, axis=-1)
    u = c_gate * u
    t = u.shape[1]
    pad = CONV_WIDTH // 2
    up = jnp.pad(u, ((0, 0), (pad, pad), (0, 0)))
    y = up[:, 0:t] * conv_w[0]
    for j in range(1, CONV_WIDTH):
        y = y + up[:, j:j + t] * conv_w[j]
    return (b_gate * y) @ w_out


def setup_inputs(seed: int = 0) -> dict:
    key = jax.random.key(seed)
    ks = jax.random.split(key, 18)

    def nrm(k, shape, scale):
        return jax.random.normal(k, shape, jnp.float32) * scale

    return {
        'x': nrm(ks[0], (BATCH, SEQ, D_MODEL), 1.0),
        'c': nrm(ks[1], (BATCH, D_MODEL), 1.0),
        'ctx': nrm(ks[2], (BATCH, CTX_LEN, D_MODEL), 1.0),
        'c_ctx': nrm(ks[3], (D_MODEL,), 1.0),
        'mod_w': nrm(ks[4], (DEPTH, D_MODEL, N_MOD * D_MODEL), 0.5 * D_MODEL ** -0.5),
        'mod_b': nrm(ks[5], (DEPTH, N_MOD * D_MODEL), 0.02),
        'norm_g': 1.0 + nrm(ks[6], (DEPTH, 3, D_MODEL), 0.05),
        'ffn_w1': nrm(ks[7], (DEPTH, 2, D_MODEL, 2 * D_FF), D_MODEL ** -0.5),
        'ffn_w2': nrm(ks[8], (DEPTH, 2, D_FF, D_MODEL), D_FF ** -0.5),
        'ab_w_in': nrm(ks[9], (N_EVEN, D_MODEL, AB_IN), D_MODEL ** -0.5),
        'ab_w_out': nrm(ks[10], (N_EVEN, AB_OUT, D_MODEL), AB_OUT ** -0.5),
        'attn_q_gain': 1.0 + nrm(ks[11], (N_EVEN, HEAD_DIM), 0.05),
        'attn_k_gain': 1.0 + nrm(ks[12], (N_EVEN, HEAD_DIM), 0.05),
        'hgrn_lb_logits': nrm(ks[13], (2, DEPTH + 1, HGRN_KEY_WIDTH), 0.1),
        'hgrn_out_gain': 1.0 + nrm(ks[14], (N_EVEN, HGRN_WIDTH // HGRN_HEADS), 0.05),
        'conv_w_in': nrm(ks[15], (N_ODD, D_MODEL, 3 * D_MODEL), D_MODEL ** -0.5),
        'conv_w': nrm(ks[16], (N_ODD, CONV_WIDTH, D_MODEL), CONV_WIDTH ** -0.5),
        'conv_w_out': nrm(ks[17], (N_ODD, D_MODEL, D_MODEL), D_MODEL ** -0.5),
    }


def reference(x, c, ctx, c_ctx, mod_w, mod_b, norm_g, ffn_w1, ffn_w2, ab_w_in, ab_w_out, attn_q_gain, attn_k_gain, hgrn_lb_logits, hgrn_out_gain, conv_w_in, conv_w, conv_w_out):
    cos, sin = axial_rope_tables(x.shape[1])
    lb_table = jnp.cumsum(jax.nn.softmax(hgrn_lb_logits.astype(jnp.float32), axis=1), axis=1)
    silu_c = jax.nn.silu(c)
    silu_cc = jax.nn.silu(c_ctx)[None]
    xc = ctx
    for layer in range(DEPTH):
        last = layer == DEPTH - 1
        even = layer % 2 == 0
        ctx_needed = even or not last
        g = norm_g[layer]
        mod = (silu_c @ mod_w[layer] + mod_b[layer]).reshape(-1, 1, N_MOD, D_MODEL)
        mod_c = (silu_cc @ mod_w[layer] + mod_b[layer]).reshape(1, 1, N_MOD, D_MODEL)
        x = x + 0.5 * mod[:, :, 2] * swiglu(adaln_in(x, g[0], mod[:, :, 0], mod[:, :, 1]), ffn_w1[layer, 0], ffn_w2[layer, 0])
        h = adaln_in(x, g[1], mod[:, :, 3], mod[:, :, 4])
        hc = None
        if ctx_needed:
            xc = xc + 0.5 * mod_c[:, :, 2] * swiglu(adaln_in(xc, g[0], mod_c[:, :, 0], mod_c[:, :, 1]), ffn_w1[layer, 0], ffn_w2[layer, 0])
            hc = adaln_in(xc, g[1], mod_c[:, :, 3], mod_c[:, :, 4])
        if even:
            e = layer // 2
            y, y_c = mixer_ab(h, hc, ab_w_in[e], ab_w_out[e], attn_q_gain[e], attn_k_gain[e], lb_table[:, layer], hgrn_out_gain[e], cos, sin, not last)
        else:
            o = layer // 2
            y = mixer_conv(h, conv_w_in[o], conv_w[o], conv_w_out[o])
            y_c = None if last else mixer_conv(hc, conv_w_in[o], conv_w[o], conv_w_out[o])
        x = x + mod[:, :, 5] * y
        x = x + 0.5 * mod[:, :, 8] * swiglu(adaln_in(x, g[2], mod[:, :, 6], mod[:, :, 7]), ffn_w1[layer, 1], ffn_w2[layer, 1])
        if not last:
            xc = xc + mod_c[:, :, 5] * y_c
            xc = xc + 0.5 * mod_c[:, :, 8] * swiglu(adaln_in(xc, g[2], mod_c[:, :, 6], mod_c[:, :, 7]), ffn_w1[layer, 1], ffn_w2[layer, 1])
    return x
```

```python
import contextlib
import numpy as np
import ml_dtypes
import concourse.bass as bass
import concourse.mybir as mybir
from concourse.bass_utils import run_bass_kernel_spmd

F32 = mybir.dt.float32
BF16 = mybir.dt.bfloat16
AF = mybir.ActivationFunctionType
ALU = mybir.AluOpType

ENGS = ("pe", "act", "dve", "pool", "sp")
DMA_NSEM = 8


class Op:
    __slots__ = ("eng", "fn", "deps", "is_dma", "idx", "has_dep", "cnt", "qi")

    def __init__(self, eng, fn, is_dma):
        self.eng = eng
        self.fn = fn
        self.is_dma = is_dma
        self.deps = ()
        self.has_dep = False
        self.cnt = None
        self.qi = None


def _phase(cond):
    if cond:
        with contextlib.ExitStack() as st:
            yield st


class _Rec:
    def __getattr__(self, name):
        return lambda *a, **k: (name, a, k)


_REC = _Rec()


def _replay(call):
    name, a, k = call
    return lambda e: getattr(e, name)(*a, **k)


class Prog:
    def __init__(self, nc):
        self.nc = nc
        self.ops = {e: [] for e in ENGS}
        self.lastw = {}
        self.readers = {}

    def _add(self, eng, fn, r, w, is_dma=False):
        op = Op(eng, fn, is_dma)
        deps = set()
        for k in r:
            lw = self.lastw.get(k)
            if lw is not None:
                deps.add(lw)
        for k in w:
            lw = self.lastw.get(k)
            if lw is not None:
                deps.add(lw)
            rd = self.readers.get(k)
            if rd:
                deps.update(rd.values())
        op.deps = tuple(deps)
        for d in op.deps:
            d.has_dep = True
        for k in w:
            self.lastw[k] = op
            self.readers[k] = {}
        rk = id(op) if is_dma else eng
        for k in r:
            if k not in w:
                self.readers.setdefault(k, {})[rk] = op
        op.idx = len(self.ops[eng])
        self.ops[eng].append(op)
        return op

    def op(self, eng, fn, r=(), w=()):
        return self._add(eng, _replay(fn(_REC)), r, w, False)

    def dma(self, q, out, in_, r=(), w=(), **kw):
        return self._add(q, lambda e: e.dma_start(out=out, in_=in_, **kw), r, w, True)

    def barrier(self):
        lasts = []
        for e in ENGS:
            ops = self.ops[e]
            for o in reversed(ops):
                if not o.is_dma and o.fn is not None:
                    lasts.append(o)
                    break
            cnt = 0
            for o in reversed(ops):
                if o.is_dma:
                    lasts.append(o)
                    cnt += 1
                    if cnt >= DMA_NSEM:
                        break
        for e in ENGS:
            op = Op(e, None, False)
            op.deps = tuple(lasts)
            for d in op.deps:
                d.has_dep = True
            op.idx = len(self.ops[e])
            self.ops[e].append(op)

    def emit(self):
        nc = self.nc
        with contextlib.ExitStack() as st:
            esem = {e: st.enter_context(nc.semaphore("s_" + e)) for e in ENGS}
            dsem = {e: [st.enter_context(nc.semaphore("d_%s%d" % (e, i))) for i in range(DMA_NSEM)]
                    for e in ("sp", "pool", "act")}
            for e in ENGS:
                c = 0
                q = 0
                for op in self.ops[e]:
                    if op.is_dma:
                        op.qi = q
                        q += 1
                    elif op.has_dep:
                        c += 1
                        op.cnt = c
            block = st.enter_context(nc.Block())

            def run(ename, eng):
                known = {}

                def wait(sem, val, key):
                    if known.get(key, 0) >= val:
                        return
                    known[key] = val
                    eng.wait_ge(sem, val)

                for op in self.ops[ename]:
                    for d in op.deps:
                        if d.is_dma:
                            wait(dsem[d.eng][d.qi % DMA_NSEM], 16 * (d.qi // DMA_NSEM + 1),
                                 ("d", d.eng, d.qi % DMA_NSEM))
                        else:
                            if d.eng == "pe" and ename == "pe":
                                continue
                            wait(esem[d.eng], d.cnt, ("e", d.eng))
                    if op.fn is None:
                        continue
                    if op.is_dma:
                        i = op.qi
                        if i >= DMA_NSEM:
                            wait(dsem[ename][i % DMA_NSEM], 16 * (i // DMA_NSEM),
                                 ("d", ename, i % DMA_NSEM))
                        op.fn(eng).then_inc(dsem[ename][i % DMA_NSEM], 16)
                    else:
                        ins = op.fn(eng)
                        if op.cnt is not None:
                            ins.then_inc(esem[ename], 1)
                nq = sum(1 for o in self.ops[ename] if o.is_dma)
                for s_ in range(min(nq, DMA_NSEM)):
                    last = ((nq - 1 - s_) // DMA_NSEM) * DMA_NSEM + s_
                    wait(dsem[ename][s_], 16 * (last // DMA_NSEM + 1), ("d", ename, s_))

            block.tensor(lambda eng: run("pe", eng))
            block.scalar(lambda eng: run("act", eng))
            block.vector(lambda eng: run("dve", eng))
            block.gpsimd(lambda eng: run("pool", eng))
            block.sync(lambda eng: run("sp", eng))


CFG = dict(D=4096, T=2048, TC=256, GRID_W=64, TT=512)
EPS = 1e-6
HD = 128
KT = 16
NWB = 3


def derive(cfg):
    c = dict(cfg)
    D = c["D"]
    c["DC"] = D // 128
    c["AW"] = D // 2
    c["NQ"] = c["AW"] // HD
    c["NKV"] = c["NQ"] // 4
    c["KVW"] = c["NKV"] * HD
    c["HW"] = D - c["AW"]
    c["HH"] = c["HW"] // HD
    c["F"] = 2 * D
    c["S"] = c["T"] + c["TC"]
    aw, kvw, hw = c["AW"], c["KVW"], c["HW"]
    offs = np.cumsum([0, aw, kvw, kvw, hw, hw, hw, hw, hw])
    c["OFF"] = dict(zip(["qa", "ka", "va", "qb", "zf", "zb", "ib", "gb", "end"], [int(o) for o in offs]))
    c["ABIN"] = int(offs[-1])
    return c


def build_nc(cfg, dbg=()):
    c = derive(cfg)
    _ORD = 'LMABCDZ'
    STOPI = _ORD.index(cfg.get('STOP', 'Z'))
    D, T, TC, S, DC, F, TT = c["D"], c["T"], c["TC"], c["S"], c["DC"], c["F"], c["TT"]
    NQ, NKV, HH, AW, HW, ABIN, OFF = c["NQ"], c["NKV"], c["HH"], c["AW"], c["HW"], c["ABIN"], c["OFF"]
    FC = F // 128
    nc = bass.Bass("TRN2", target_bir_lowering=False)
    P = Prog(nc)

    def din(name, shape, dt=F32):
        return nc.dram_tensor(name, list(shape), dt, kind="ExternalInput").ap()

    x_in = din("x", [T, D]); ctx_in = din("ctx", [TC, D]); cvec = din("cvec", [2, D])
    mod_w = din("mod_w", [2, D, 9 * D]); mod_b = din("mod_b", [2, 9 * D]); norm_g = din("norm_g", [2, 3, D])
    ffn_w1 = din("ffn_w1", [2, 2, D, 2 * F]); ffn_w2 = din("ffn_w2", [2, 2, F, D])
    ab_w_in = din("ab_w_in", [1, D, ABIN]); ab_w_out = din("ab_w_out", [1, D, D])
    q_gain = din("attn_q_gain", [1, HD]); k_gain = din("attn_k_gain", [1, HD])
    lb_logits = din("hgrn_lb_logits", [2, 3, HW]); out_gain = din("hgrn_out_gain", [1, HD])
    conv_w_in = din("conv_w_in", [1, D, 3 * D]); conv_w = din("conv_w", [1, 3, D]); conv_w_out = din("conv_w_out", [1, D, D])
    k_ropec = din("k_ropec", [128, T]); k_ropes = din("k_ropes", [128, T])
    k_perm = din("k_perm", [128, 128], BF16); k_ident = din("k_ident", [128, 128])
    k_identb = din("k_identb", [128, 128], BF16)
    k_maskf = din("k_maskf", [128, 128]); k_maskb = din("k_maskb", [128, 128])
    out = nc.dram_tensor("out", [T, D], F32, kind="ExternalOutput").ap()

    def dscr(name, shape, dt=F32):
        kind = "ExternalOutput" if name in dbg else "Internal"
        return nc.dram_tensor(name, list(shape), dt, kind=kind).ap()

    xT = dscr("xT", [D, S])
    projT = dscr("projT", [ABIN, S])
    catT = dscr("catT", [D, T], BF16)
    cbT = dscr("cbT", [D, T])
    cuT = dscr("cuT", [D, T])

    tiles = [(i * TT, TT) for i in range(T // TT)]
    ctile = (T, TC)

    with contextlib.ExitStack() as top:
        def sb(st, name, shape, dt=F32):
            return st.enter_context(nc.sbuf_tensor(name, list(shape), dt))

        ones32 = sb(top, "ones32", [128, 128]); onesb = sb(top, "onesb", [128, 128], BF16)
        ident = sb(top, "ident", [128, 128]); identb = sb(top, "identb", [128, 128], BF16)
        epsc = sb(top, "epsc", [128, 1])
        modT = sb(top, "modT", [128, 2, 9 * DC, 2])
        modbT = sb(top, "modbT", [128, 2, 9 * DC])
        ngT = sb(top, "ngT", [128, 6, DC, 1])
        GS = sb(top, "GS", [128, 2, 3, 2, DC, 2])
        RS = sb(top, "RS", [128, 2, 3, DC, 2])
        rstd = sb(top, "rstd", [128, TT])
        SQH = [sb(top, "sqh%d" % i, [128, TT], BF16) for i in range(2)]
        SQL = [sb(top, "sql%d" % i, [128, TT], BF16) for i in range(2)]
        qg_ = sb(top, "qgain", [128, 1]); kg_ = sb(top, "kgain", [128, 1]); og_ = sb(top, "ogain", [128, 1])
        LB = sb(top, "LB", [128, 2, 3, HH]); cwT = sb(top, "cwT", [128, 3, DC])
        craw = sb(top, "craw", [128, 2, DC])
        ps = [top.enter_context(nc.psum_tensor("ps%d" % i, [128, 512], F32)) for i in range(8)]
        psb6 = ps[6]
        psb7 = ps[7]
        PSK = [("ps", i) for i in range(6)]
        PS5 = [("ps", 5)]
        TBK = [(ps[5], ("ps", 5)), (ps[6], "psb6")]

        P.op("dve", lambda e: e.memset(ones32[:], 1.0), w=["ones32"])
        P.op("dve", lambda e: e.memset(onesb[:], 1.0), w=["onesb"])
        P.op("dve", lambda e: e.memset(epsc[:], EPS), w=["epsc"])
        P.dma("sp", ident[:], k_ident, w=["ident"])
        P.dma("sp", identb[:], k_identb, w=["identb"])
        rowt = [sb(top, "rowt%d" % i, [128, 128]) for i in range(2)]
        rtc = [0]

        def load_rows_T(dst2d, src, nrows, key):
            for r0 in range(0, nrows, 128):
                rows = min(128, nrows - r0)
                i = rtc[0] % 2
                rtc[0] += 1
                if rows < 128:
                    P.op("dve", lambda e: e.memset(rowt[i][:], 0.0), w=[("rowt", i)])
                P.dma("sp", rowt[i][0:rows, :], src[r0:r0 + rows, :], w=[("rowt", i)])
                P.op("pe", lambda e: e.matmul(ps[5][:, 0:128], rowt[i][:], ident[:], start=True, stop=True),
                     r=[("rowt", i), "ident"], w=PS5)
                P.op("act", lambda e: e.activation(out=dst2d[:, r0:r0 + rows], in_=ps[5][:, 0:rows], func=AF.Copy),
                     r=PS5, w=[key])

        import os as _os
        load_rows_T(modbT[:].rearrange("p l n -> p (l n)"), mod_b.rearrange("l (n p) -> (l n) p", p=128), 2 * 9 * DC, "modbT")
        load_rows_T(ngT[:].rearrange("p a c o -> p (a c o)"), norm_g.rearrange("l j (c p) -> (l j c) p", p=128), 6 * DC, "ngT")
        load_rows_T(craw[:].rearrange("p m c -> p (m c)"), cvec.rearrange("m (c p) -> (m c) p", p=128), 2 * DC, "craw")
        load_rows_T(qg_[:], q_gain, 1, "qgain")
        load_rows_T(kg_[:], k_gain, 1, "kgain")
        load_rows_T(og_[:], out_gain, 1, "ogain")
        load_rows_T(LB[:].rearrange("p d l h -> p (d l h)"), lb_logits.rearrange("d l (h p) -> (d l h) p", p=128), 6 * HH, "LB")
        load_rows_T(cwT[:].rearrange("p j c -> p (j c)"), conv_w.rearrange("o j (c p) -> (o j c) p", p=128), 3 * DC, "cwT")

        def split2_mm(bank_ap, bank_keys, lhsT_b, lkey, src, srckey, n_, hi, hik, lo, lok, start, stop):
            P.op("dve", lambda e: e.tensor_copy(out=hi, in_=src), r=[srckey], w=[hik])
            P.op("dve", lambda e: e.tensor_tensor(out=lo, in0=src, in1=hi, op=ALU.subtract), r=[srckey, hik], w=[lok])
            P.op("pe", lambda e: e.matmul(bank_ap, lhsT_b, hi, start=start, stop=False), r=[hik, lkey], w=bank_keys)
            P.op("pe", lambda e: e.matmul(bank_ap, lhsT_b, lo, start=False, stop=stop), r=[lok, lkey], w=bank_keys)

        wctr = [0]

        def gemm(st_wb, X, xkey, KC, Tt, blocks, epi):
            WB = st_wb
            nkt = (KC + KT - 1) // KT
            sched = [(bi, kt) for bi in range(len(blocks)) for kt in range(nkt)]

            def load(i):
                bi, kt = sched[i]
                slot = (wctr[0] + i) % NWB
                k0 = kt * KT
                kn = min(KT, KC - k0)
                co = 0
                for si, seg in enumerate(blocks[bi]):
                    wdt = seg.shape[1]
                    P.dma("pool", WB[slot][:, 0:kn, co:co + wdt],
                          seg[k0 * 128:(k0 + kn) * 128, :].rearrange("(c p) n -> p c n", p=128),
                          w=[("wb", slot, si)])
                    co += wdt

            PF = NWB - 1
            for i in range(min(PF, len(sched))):
                load(i)
            for i, (bi, kt) in enumerate(sched):
                if i + PF < len(sched):
                    load(i + PF)
                slot = (wctr[0] + i) % NWB
                k0 = kt * KT
                kn = min(KT, KC - k0)
                ncols = sum(seg.shape[1] for seg in blocks[bi])
                wkeys = [("wb", slot, si) for si in range(len(blocks[bi]))]
                nch = ncols // 128
                for j in range(nch):
                    for k in range(kn):
                        kk = k0 + k
                        P.op("pe", lambda e, j=j, k=k, kk=kk, slot=slot: e.matmul(
                            ps[j][:, 0:Tt], WB[slot][:, k, j * 128:(j + 1) * 128], X[:, kk, 0:Tt],
                            start=(kk == 0), stop=(kk == KC - 1)),
                            r=[xkey(kk)] + wkeys, w=[PSK[j]])
                if kt == nkt - 1:
                    for j in range(nch):
                        epi(bi, j, ps[j][:, 0:Tt], PSK[j])
            wctr[0] += len(sched)

        def colblocks(Wap, c0, c1, width=512):
            return [[Wap[:, a:min(a + width, c1)]] for a in range(c0, c1, width)]

        def stats_tail(n, nlast, src, srckey, Tt, SQ):
            sq = SQ[n % 2]
            P.op("act", lambda e: e.activation(out=sq[:, 0:Tt], in_=src, func=AF.Square),
                 r=[srckey], w=[("sq", n % 2)])
            split2_mm(ps[4][:, 0:Tt], [PSK[4]], onesb[:], "onesb", sq[:, 0:Tt], ("sq", n % 2), Tt,
                      SQH[n % 2][:, 0:Tt], ("sqh", n % 2), SQL[n % 2][:, 0:Tt], ("sql", n % 2), n == 0, n == nlast)
            if n == nlast:
                P.op("act", lambda e: e.activation(out=rstd[:, 0:Tt], in_=ps[4][:, 0:Tt], func=AF.Sqrt,
                                                   scale=1.0 / D, bias=epsc[:, 0:1]),
                     r=[PSK[4], "epsc"], w=["rstd"])
                P.op("dve", lambda e: e.reciprocal(out=rstd[:, 0:Tt], in_=rstd[:, 0:Tt]), r=["rstd"], w=["rstd"])

        def resid_epilogue(c0, Tt, svec, XC, XO, SQ, nchunks=None):
            nchunks = DC if nchunks is None else nchunks

            def ldx(n):
                P.dma("sp", XC[n % 4][:, 0:Tt], xT[n * 128:(n + 1) * 128, c0:c0 + Tt],
                      r=[("xT", n, c0)], w=[("xc", n % 4)])

            def epi(bi, j, pap, pk):
                n = bi * 4 + j
                if n == 0:
                    for m in range(min(4, nchunks)):
                        ldx(m)
                xo = XO[n % 4]
                P.op("dve", lambda e: e.scalar_tensor_tensor(out=xo[:, 0:Tt], in0=pap, scalar=svec(n),
                                                             in1=XC[n % 4][:, 0:Tt], op0=ALU.mult, op1=ALU.add),
                     r=[pk, ("xc", n % 4), "RS"], w=[("xo", n % 4)])
                P.dma("sp", xT[n * 128:(n + 1) * 128, c0:c0 + Tt], xo[:, 0:Tt], r=[("xo", n % 4)], w=[("xT", n, c0)])
                stats_tail(n, nchunks - 1, xo[:, 0:Tt], ("xo", n % 4), Tt, SQ)
                if n + 4 < nchunks:
                    ldx(n + 4)
            return epi

        def adaln(c0, Tt, l, j, col, H, XC, TMP):
            for n in range(DC):
                P.dma("sp", XC[n % 4][:, 0:Tt], xT[n * 128:(n + 1) * 128, c0:c0 + Tt],
                      r=[("xT", n, c0)], w=[("xc", n % 4)])
                t = TMP[n % 2]
                P.op("dve", lambda e, n=n, t=t: e.tensor_tensor(out=t[:, 0:Tt], in0=XC[n % 4][:, 0:Tt], in1=rstd[:, 0:Tt],
                                                                op=ALU.mult),
                     r=[("xc", n % 4), "rstd"], w=[("tmp", n % 2)])
                P.op("act", lambda e, n=n, t=t: e.activation(out=H[:, n, 0:Tt], in_=t[:, 0:Tt], func=AF.Identity,
                                                             scale=GS[:, l, j, 0, n, col:col + 1],
                                                             bias=GS[:, l, j, 1, n, col:col + 1]),
                     r=[("tmp", n % 2), "GS"], w=[("H", n)])

        def ffn(WB, c0, Tt, l, i, col, H, A, XC, XO, SQ, TMP, SG):
            adaln(c0, Tt, l, 2 * i, col, H, XC, TMP)
            w1 = ffn_w1[l, i]
            w2 = ffn_w2[l, i]
            blocks = [[w1[:, j0 * 128:j0 * 128 + 256], w1[:, F + j0 * 128:F + j0 * 128 + 256]] for j0 in range(0, FC, 2)]

            def epi1(bi, j, pap, pk):
                jj = j % 2
                if j < 2:
                    P.op("act", lambda e: e.activation(out=SG[jj][:, 0:Tt], in_=pap, func=AF.Silu),
                         r=[pk], w=[("sg", jj)])
                else:
                    a = bi * 2 + jj
                    P.op("dve", lambda e: e.tensor_tensor(out=A[:, a, 0:Tt], in0=SG[jj][:, 0:Tt], in1=pap, op=ALU.mult),
                         r=[pk, ("sg", jj)], w=[("A", a)])
            gemm(WB, H, lambda k: ("H", k), DC, Tt, blocks, epi1)
            gemm(WB, A, lambda k: ("A", k), FC, Tt, colblocks(w2, 0, D),
                 resid_epilogue(c0, Tt, lambda n: RS[:, l, 2 * i, n, col:col + 1], XC, XO, SQ))

        for st in _phase(STOPI >= _ORD.index('M') and not _os.environ.get('NOM')):
            P.barrier()
            WB = [sb(st, "wbm%d" % i, [128, KT, 512], BF16) for i in range(NWB)]
            cs_ = sb(st, "cs", [128, 2, DC], BF16)
            P.op("act", lambda e: e.activation(out=cs_[:], in_=craw[:], func=AF.Silu), r=["craw"], w=["cs"])
            cs = cs_[:].rearrange("p m c -> p c m")
            for l in range(2):
                def epim(bi, j, pap, pk, l=l):
                    n = bi * 4 + j
                    P.op("dve", lambda e: e.tensor_scalar(out=modT[:, l, n, :], in0=pap, scalar1=modbT[:, l, n:n + 1],
                                                          scalar2=None, op0=ALU.add),
                         r=[pk, "modbT"], w=["modT"])
                gemm(WB, cs, lambda k: "cs", DC, 2, colblocks(mod_w[l], 0, 9 * D), epim)
            for l in range(2):
                for j in range(3):
                    sh = modT[:, l, (3 * j) * DC:(3 * j + 1) * DC, :]
                    sc = modT[:, l, (3 * j + 1) * DC:(3 * j + 2) * DC, :]
                    gt = modT[:, l, (3 * j + 2) * DC:(3 * j + 3) * DC, :]
                    P.op("dve", lambda e, l=l, j=j, sc=sc: e.scalar_tensor_tensor(
                        out=GS[:, l, j, 0, :, :], in0=sc, scalar=1.0,
                        in1=ngT[:, l * 3 + j, :, :].to_broadcast([128, DC, 2]), op0=ALU.add, op1=ALU.mult),
                        r=["modT", "ngT"], w=["GS"])
                    P.op("dve", lambda e, l=l, j=j, sh=sh: e.tensor_copy(out=GS[:, l, j, 1, :, :], in_=sh),
                         r=["modT"], w=["GS"])
                    fac = 1.0 if j == 1 else 0.5
                    P.op("dve", lambda e, l=l, j=j, gt=gt, fac=fac: e.tensor_scalar(
                        out=RS[:, l, j, :, :], in0=gt, scalar1=fac, scalar2=None, op0=ALU.mult),
                        r=["modT"], w=["RS"])

        for st in _phase(STOPI >= _ORD.index('A')):
            P.barrier()
            WB = [sb(st, "wb%d" % i, [128, KT, 512], BF16) for i in range(NWB)]
            H = sb(st, "H", [128, DC, TT], BF16)
            A = sb(st, "A", [128, FC, TT], BF16)
            XC = [sb(st, "xc%d" % i, [128, TT]) for i in range(4)]
            XO = [sb(st, "xo%d" % i, [128, TT]) for i in range(4)]
            SQ = [sb(st, "sq%d" % i, [128, TT]) for i in range(2)]
            TMP = [sb(st, "tmp%d" % i, [128, TT]) for i in range(2)]
            SG = [sb(st, "sg%d" % i, [128, TT]) for i in range(2)]
            XIN = [sb(st, "xin%d" % i, [128, 512]) for i in range(2)]
            XR1 = sb(st, "xr1", [128, 512])
            XS = [sb(st, "xs%d" % i, [128, 512], BF16) for i in range(3)]

            def transpose_in(src, r0, c0, Tt):
                nb = Tt // 128
                pc = 0
                for tb in range(nb):
                    for n0 in range(0, DC, 4):
                        xb = XIN[pc % 2]
                        xk = ("xin", pc % 2)
                        bank, bkey = TBK[pc % 2]
                        xo = XO[pc % 4]
                        xok = ("xo", pc % 4)
                        pc += 1
                        P.dma("sp", xb[:], src[r0 + tb * 128:r0 + (tb + 1) * 128, n0 * 128:(n0 + 4) * 128], w=[xk])
                        P.op("act", lambda e: e.activation(out=XS[0][:], in_=xb[:], func=AF.Copy), r=[xk], w=["xs0"])
                        P.op("dve", lambda e: e.tensor_tensor(out=XR1[:], in0=xb[:], in1=XS[0][:], op=ALU.subtract), r=[xk, "xs0"], w=["xr1"])
                        P.op("act", lambda e: e.activation(out=XS[1][:], in_=XR1[:], func=AF.Copy), r=["xr1"], w=["xs1"])
                        P.op("dve", lambda e: e.tensor_tensor(out=XS[2][:], in0=XR1[:], in1=XS[1][:], op=ALU.subtract), r=["xr1", "xs1"], w=["xs2"])
                        for sl in range(4):
                            for q3 in range(3):
                                P.op("pe", lambda e: e.matmul(bank[:, sl * 128:(sl + 1) * 128], XS[q3][:, sl * 128:(sl + 1) * 128], identb[:],
                                                              start=(q3 == 0), stop=(q3 == 2)),
                                     r=["xs%d" % q3, "identb"], w=[bkey])
                        P.op("act", lambda e: e.activation(out=xo[:, 0:512], in_=bank[:, 0:512], func=AF.Copy), r=[bkey], w=[xok])
                        for sl in range(4):
                            n = n0 + sl
                            P.dma("sp", xT[n * 128:(n + 1) * 128, c0 + tb * 128:c0 + (tb + 1) * 128], xo[:, sl * 128:(sl + 1) * 128],
                                  r=[xok], w=[("xT", n, c0)])
                for n in range(0 if not _os.environ.get("A_NOSTAT") else DC, DC):
                    P.dma("sp", XC[n % 4][:, 0:Tt], xT[n * 128:(n + 1) * 128, c0:c0 + Tt],
                          r=[("xT", n, c0)], w=[("xc", n % 4)])
                    stats_tail(n, DC - 1, XC[n % 4][:, 0:Tt], ("xc", n % 4), Tt, SQ)

            win = ab_w_in[0]
            for ti, (c0, Tt) in enumerate(tiles + [ctile]):
                if ti >= int(_os.environ.get("A_TILES", "99")):
                    continue
                is_ctx = ti == len(tiles)
                col = 1 if is_ctx else 0
                ASTOP = int(_os.environ.get("ASTOP", "9"))
                transpose_in(ctx_in if is_ctx else x_in, 0 if is_ctx else c0, c0, Tt)
                if ASTOP < 2:
                    continue
                ffn(WB, c0, Tt, 0, 0, col, H, A, XC, XO, SQ, TMP, SG)
                if ASTOP < 3:
                    continue
                adaln(c0, Tt, 0, 1, col, H, XC, TMP)
                if is_ctx:
                    rng = [(OFF["ka"], OFF["qb"]), (OFF["zf"], OFF["gb"])]
                else:
                    rng = [(0, ABIN)]
                for (a0, a1) in rng:
                    blocks = colblocks(win, a0, a1)

                    def epip(bi, j, pap, pk, a0=a0, c0=c0, Tt=Tt):
                        n = bi * 4 + j
                        xo = XO[n % 4]
                        P.op("act", lambda e: e.activation(out=xo[:, 0:Tt], in_=pap, func=AF.Copy),
                             r=[pk], w=[("xo", n % 4)])
                        P.dma("sp", projT[a0 + n * 128:a0 + (n + 1) * 128, c0:c0 + Tt], xo[:, 0:Tt],
                              r=[("xo", n % 4)], w=[("projT", a0 + n * 128, c0)])
                    gemm(WB, H, lambda k: ("H", k), DC, Tt, blocks, epip)

        def proj_keys(row0):
            return [("projT", row0, c0) for (c0, _t) in tiles + [ctile]]

        SC = float(HD) ** -0.5
        for st in _phase(STOPI >= _ORD.index('B')):
            P.barrier()
            ropec = sb(st, "ropec", [128, T]); ropes = sb(st, "ropes", [128, T]); perm = sb(st, "perm", [128, 128], BF16)
            QGH = sb(st, "aqgh", [128, TT], BF16); QGL = sb(st, "aqgl", [128, TT], BF16)
            P.dma("sp", ropec[:], k_ropec, w=["ropec"]); P.dma("sp", ropes[:], k_ropes, w=["ropes"])
            P.dma("sp", perm[:], k_perm, w=["perm"])
            KR = sb(st, "KR", [128, S], BF16)
            VT = sb(st, "VT", [128, S // 128, 128], BF16)
            QR = [sb(st, "QR%d" % i, [128, TT], BF16) for i in range(2)]
            RAW = [sb(st, "raw%d" % i, [128, TT]) for i in range(2)]
            SQ = sb(st, "asq", [128, TT]); QG = sb(st, "aqg", [128, TT]); RT = sb(st, "art", [128, TT])
            T1 = sb(st, "at1", [128, TT]); T2 = sb(st, "at2", [128, TT])
            VB = sb(st, "avb", [128, TT], BF16)
            PT = [sb(st, "PT%d" % i, [128, TT], BF16) for i in range(3)]
            RSUM = sb(st, "arsum", [128, TT]); OB = [sb(st, "aob%d" % i, [128, TT], BF16) for i in range(2)]
            rawc = [0]

            def load_raw(row0, c0, Tt):
                i = rawc[0] % 2
                rawc[0] += 1
                P.dma("sp", RAW[i][:, 0:Tt], projT[row0:row0 + 128, c0:c0 + Tt], r=[("projT", row0, c0)], w=[("raw", i)])
                return RAW[i], ("raw", i)

            def norm_rope(raw, rk, gain, gk, c0, Tt, rope, dst, dk):
                P.op("act", lambda e: e.activation(out=SQ[:, 0:Tt], in_=raw[:, 0:Tt], func=AF.Square), r=[rk], w=["asq"])
                P.op("dve", lambda e: e.tensor_scalar(out=QG[:, 0:Tt], in0=raw[:, 0:Tt], scalar1=gain[:, 0:1], scalar2=None,
                                                      op0=ALU.mult), r=[rk, gk], w=["aqg"])
                split2_mm(ps[4][:, 0:Tt], [PSK[4]], onesb[:], "onesb", SQ[:, 0:Tt], "asq", Tt,
                          SQH[0][:, 0:Tt], ("sqh", 0), SQL[0][:, 0:Tt], ("sql", 0), True, True)
                P.op("act", lambda e: e.activation(out=RT[:, 0:Tt], in_=ps[4][:, 0:Tt], func=AF.Sqrt, scale=1.0 / HD,
                                                   bias=epsc[:, 0:1]), r=[PSK[4], "epsc"], w=["art"])
                P.op("dve", lambda e: e.reciprocal(out=RT[:, 0:Tt], in_=RT[:, 0:Tt]), r=["art"], w=["art"])
                if rope:
                    split2_mm(ps[5][:, 0:Tt], PS5, perm[:], "perm", QG[:, 0:Tt], "aqg", Tt,
                              QGH[:, 0:Tt], "aqgh", QGL[:, 0:Tt], "aqgl", True, True)
                    P.op("dve", lambda e: e.tensor_tensor(out=T1[:, 0:Tt], in0=QG[:, 0:Tt], in1=ropec[:, c0:c0 + Tt], op=ALU.mult),
                         r=["aqg", "ropec"], w=["at1"])
                    P.op("dve", lambda e: e.tensor_tensor(out=T2[:, 0:Tt], in0=ps[5][:, 0:Tt], in1=ropes[:, c0:c0 + Tt], op=ALU.mult),
                         r=PS5 + ["ropes"], w=["at2"])
                    P.op("dve", lambda e: e.tensor_tensor(out=T1[:, 0:Tt], in0=T1[:, 0:Tt], in1=T2[:, 0:Tt], op=ALU.add),
                         r=["at1", "at2"], w=["at1"])
                    src, sk = T1, "at1"
                else:
                    src, sk = QG, "aqg"
                P.op("dve", lambda e: e.tensor_tensor(out=dst, in0=src[:, 0:Tt], in1=RT[:, 0:Tt], op=ALU.mult),
                     r=[sk, "art"], w=[dk])

            for g in range(NKV):
                for (c0, Tt) in tiles + [ctile]:
                    raw, rk = load_raw(OFF["ka"] + g * 128, c0, Tt)
                    norm_rope(raw, rk, kg_, "kgain", c0, Tt, c0 < T, KR[:, c0:c0 + Tt], "KR")
                    raw, rk = load_raw(OFF["va"] + g * 128, c0, Tt)
                    P.op("act", lambda e, raw=raw, Tt=Tt: e.activation(out=VB[:, 0:Tt], in_=raw[:, 0:Tt], func=AF.Copy),
                         r=[rk], w=["avb"])
                    for tb in range(Tt // 128):
                        kb = c0 // 128 + tb
                        P.op("pe", lambda e, tb=tb: e.matmul(psb6[:, 0:128], VB[:, tb * 128:(tb + 1) * 128], identb[:], start=True, stop=True),
                             r=["avb", "identb"], w=["psb6"])
                        P.op("dve", lambda e, kb=kb: e.tensor_copy(out=VT[:, kb, :], in_=psb6[:, 0:128]),
                             r=["psb6"], w=["VT"])
                for hq in range(4):
                    h = g * 4 + hq
                    for qi, (c0, Tt) in enumerate(tiles):
                        raw, rk = load_raw(OFF["qa"] + h * 128, c0, Tt)
                        qr = QR[qi % 2]
                        norm_rope(raw, rk, qg_, "qgain", c0, Tt, True, qr[:, 0:Tt], ("QR", qi % 2))
                        nkb = S // 128

                        def smm(kb, qr=qr, qi=qi, Tt=Tt):
                            P.op("pe", lambda e: e.matmul(ps[kb % 2][:, 0:Tt], KR[:, kb * 128:(kb + 1) * 128], qr[:, 0:Tt],
                                                          start=True, stop=True),
                                 r=["KR", ("QR", qi % 2)], w=[PSK[kb % 2]])
                        smm(0)
                        for kb in range(nkb):
                            pt = PT[kb % 3]
                            P.op("act", lambda e, kb=kb, pt=pt, Tt=Tt: e.activation(out=pt[:, 0:Tt], in_=ps[kb % 2][:, 0:Tt],
                                                                                    func=AF.Exp, scale=SC),
                                 r=[PSK[kb % 2]], w=[("PT", kb % 3)])
                            if kb + 1 < nkb:
                                smm(kb + 1)
                            P.op("pe", lambda e, kb=kb, pt=pt, Tt=Tt: e.matmul(ps[2][:, 0:Tt], VT[:, kb, :], pt[:, 0:Tt],
                                                                               start=(kb == 0), stop=(kb == nkb - 1)),
                                 r=["VT", ("PT", kb % 3)], w=[PSK[2]])
                            P.op("pe", lambda e, kb=kb, pt=pt, Tt=Tt: e.matmul(ps[3][:, 0:Tt], onesb[:], pt[:, 0:Tt],
                                                                               start=(kb == 0), stop=(kb == nkb - 1)),
                                 r=["onesb", ("PT", kb % 3)], w=[PSK[3]])
                        P.op("dve", lambda e, Tt=Tt: e.reciprocal(out=RSUM[:, 0:Tt], in_=ps[3][:, 0:Tt]), r=[PSK[3]], w=["arsum"])
                        ob = OB[qi % 2]
                        P.op("dve", lambda e, ob=ob, Tt=Tt: e.tensor_tensor(out=ob[:, 0:Tt], in0=ps[2][:, 0:Tt], in1=RSUM[:, 0:Tt],
                                                                            op=ALU.mult),
                             r=[PSK[2], "arsum"], w=[("aob", qi % 2)])
                        P.dma("sp", catT[h * 128:(h + 1) * 128, c0:c0 + Tt], ob[:, 0:Tt], r=[("aob", qi % 2)],
                              w=[("catT", h, c0)])

        for st in _phase(STOPI >= _ORD.index('C')):
            P.barrier()
            lbv = sb(st, "lbv", [128, 2, HH]); oml = sb(st, "oml", [128, 2, HH])
            noml = sb(st, "noml", [128, 2, HH]); lsum = sb(st, "lsum", [128, 2, HH])
            maskf = sb(st, "maskf", [128, 128]); maskb = sb(st, "maskb", [128, 128]); rmask = sb(st, "rmask", [128, TT])
            P.dma("sp", maskf[:], k_maskf, w=["maskf"]); P.dma("sp", maskb[:], k_maskb, w=["maskb"])
            P.op("act", lambda e: e.activation(out=LB[:], in_=LB[:], func=AF.Exp), r=["LB"], w=["LB"])
            P.op("dve", lambda e: e.tensor_tensor(out=lsum[:], in0=LB[:, :, 0, :], in1=LB[:, :, 1, :], op=ALU.add), r=["LB"], w=["lsum"])
            P.op("dve", lambda e: e.tensor_tensor(out=lsum[:], in0=lsum[:], in1=LB[:, :, 2, :], op=ALU.add), r=["LB", "lsum"], w=["lsum"])
            P.op("dve", lambda e: e.reciprocal(out=lsum[:], in_=lsum[:]), r=["lsum"], w=["lsum"])
            P.op("dve", lambda e: e.tensor_tensor(out=lbv[:], in0=LB[:, :, 0, :], in1=lsum[:], op=ALU.mult), r=["LB", "lsum"], w=["lbv"])
            P.op("dve", lambda e: e.tensor_scalar(out=oml[:], in0=lbv[:], scalar1=-1.0, scalar2=1.0, op0=ALU.mult, op1=ALU.add),
                 r=["lbv"], w=["oml"])
            P.op("dve", lambda e: e.tensor_scalar(out=noml[:], in0=oml[:], scalar1=-1.0, scalar2=None, op0=ALU.mult),
                 r=["oml"], w=["noml"])
            P.op("dve", lambda e: e.memset(rmask[:], 1.0), w=["rmask"])
            P.op("dve", lambda e: e.memset(rmask[:].rearrange("p (c l) -> p c l", l=32)[:, :, 0:1], 0.0), w=["rmask"])

            OACC = sb(st, "OACC", [128, T])
            QS = sb(st, "QS", [128, T])
            VTK = sb(st, "VTK", [128, S // 128, 128], BF16)
            VCH = sb(st, "VCH", [32, S // 32, 128], BF16)
            Z = [sb(st, "hz%d" % i, [128, TT]) for i in range(2)]
            SGm = sb(st, "hsg", [128, TT]); LF = sb(st, "hlf", [128, TT]); KK = sb(st, "hk", [128, TT])
            CUM = sb(st, "hcum", [128, TT]); CC = sb(st, "hcc", [128, TT]); EQ = sb(st, "heq", [128, TT]); EK = sb(st, "hek", [128, TT])
            DEC = [sb(st, "hdec%d" % i, [128, TT // 32, 1]) for i in range(2)]
            QP = [sb(st, "hqp%d" % i, [128, TT], BF16) for i in range(2)]
            KTL = [sb(st, "hkt%d" % i, [128, TT], BF16) for i in range(2)]
            KP = sb(st, "hkp", [128, TT]); KPB = sb(st, "hkpb", [128, TT], BF16)
            KPT = [sb(st, "hkpt%d" % i, [32, TT // 32, 128], BF16) for i in range(2)]
            AM = [sb(st, "ham%d" % i, [128, 128], BF16) for i in range(2)]
            S32 = sb(st, "hS32", [128, 128]); SBF = sb(st, "hSbf", [128, 128], BF16)
            RAWH = [sb(st, "hraw%d" % i, [128, TT]) for i in range(2)]
            VBh = sb(st, "hvb", [128, TT], BF16)
            GB = sb(st, "hgb", [128, TT]); Y1 = sb(st, "hy1", [128, TT]); YB = [sb(st, "hyb%d" % i, [128, TT], BF16) for i in range(2)]
            hc = [0]

            def hload(row0, c0, Tt):
                i = hc[0] % 2
                hc[0] += 1
                P.dma("sp", RAWH[i][:, 0:Tt], projT[row0:row0 + 128, c0:c0 + Tt], r=[("projT", row0, c0)], w=[("hraw", i)])
                return RAWH[i], ("hraw", i)

            seq_tiles = [ctile] + tiles

            for h in range(HH):
                for (c0, Tt) in tiles + [ctile]:
                    if c0 < T:
                        raw, rk = hload(OFF["qb"] + h * 128, c0, Tt)
                        P.op("act", lambda e, raw=raw, c0=c0, Tt=Tt: e.activation(out=QS[:, c0:c0 + Tt], in_=raw[:, 0:Tt], func=AF.Silu),
                             r=[rk], w=["QS"])
                    raw, rk = hload(OFF["ib"] + h * 128, c0, Tt)
                    P.op("act", lambda e, raw=raw, Tt=Tt: e.activation(out=VBh[:, 0:Tt], in_=raw[:, 0:Tt], func=AF.Copy), r=[rk], w=["hvb"])
                    for tb in range(Tt // 128):
                        kb = c0 // 128 + tb
                        P.op("pe", lambda e, tb=tb: e.matmul(psb6[:, 0:128], VBh[:, tb * 128:(tb + 1) * 128], identb[:], start=True, stop=True),
                             r=["hvb", "identb"], w=["psb6"])
                        P.op("dve", lambda e, kb=kb: e.tensor_copy(out=VTK[:, kb, :], in_=psb6[:, 0:128]), r=["psb6"], w=["VTK"])
                        for cc_ in range(4):
                            P.op("pe", lambda e, tb=tb, cc_=cc_: e.matmul(psb7[0:32, cc_ * 128:(cc_ + 1) * 128],
                                VBh[:, tb * 128 + cc_ * 32:tb * 128 + (cc_ + 1) * 32], identb[:], start=True, stop=True),
                                r=["hvb", "identb"], w=["psb7"])
                        P.op("act", lambda e, kb=kb: e.activation(
                            out=VCH[:, kb * 4:(kb + 1) * 4, :],
                            in_=psb7[0:32, 0:512].rearrange("p (c d) -> p c d", d=128), func=AF.Copy),
                            r=["psb7"], w=["VCH"])
                for d in range(2):
                    zrow = (OFF["zf"] if d == 0 else OFF["zb"]) + h * 128
                    order = seq_tiles if d == 0 else [ctile] + tiles[::-1]
                    mask = maskf if d == 0 else maskb
                    mk = "maskf" if d == 0 else "maskb"
                    P.op("dve", lambda e: e.memset(S32[:], 0.0), w=["hS32"])
                    P.op("dve", lambda e: e.memset(SBF[:], 0.0), w=["hSbf"])
                    for ti, (c0, Tt) in enumerate(order):
                        lat = c0 < T
                        nch = Tt // 32
                        b2 = ti % 2
                        raw, rk = hload(zrow, c0, Tt)
                        lbA = lbv[:, d, h:h + 1]; omA = oml[:, d, h:h + 1]; nomA = noml[:, d, h:h + 1]
                        P.op("act", lambda e, raw=raw, Tt=Tt: e.activation(out=SGm[:, 0:Tt], in_=raw[:, 0:Tt], func=AF.Sigmoid), r=[rk], w=["hsg"])
                        P.op("act", lambda e, Tt=Tt, omA=omA, lbA=lbA: e.activation(out=LF[:, 0:Tt], in_=SGm[:, 0:Tt], func=AF.Ln, scale=omA, bias=lbA),
                             r=["hsg", "oml", "lbv"], w=["hlf"])
                        P.op("dve", lambda e, Tt=Tt, omA=omA, nomA=nomA: e.tensor_scalar(out=KK[:, 0:Tt], in0=SGm[:, 0:Tt], scalar1=nomA, scalar2=omA,
                                                                                         op0=ALU.mult, op1=ALU.add),
                             r=["hsg", "oml", "noml"], w=["hk"])
                        P.op("dve", lambda e, Tt=Tt: e.tensor_tensor_scan(out=CUM[:, 0:Tt], data0=rmask[:, 0:Tt], data1=LF[:, 0:Tt], initial=0.0,
                                                                          op0=ALU.mult, op1=ALU.add),
                             r=["hlf", "rmask"], w=["hcum"])
                        cum3 = CUM[:, 0:Tt].rearrange("p (c l) -> p c l", l=32)
                        tot = cum3[:, :, 31:32]
                        if d == 0:
                            ccsrc, cck = CUM, "hcum"
                        else:
                            P.op("dve", lambda e, Tt=Tt, cum3=cum3, tot=tot, nch=nch: e.tensor_tensor(
                                out=CC[:, 0:Tt].rearrange("p (c l) -> p c l", l=32), in0=tot.to_broadcast([128, nch, 32]), in1=cum3,
                                op=ALU.subtract), r=["hcum"], w=["hcc"])
                            P.op("dve", lambda e, Tt=Tt: e.tensor_tensor(out=CC[:, 0:Tt], in0=CC[:, 0:Tt], in1=LF[:, 0:Tt], op=ALU.add),
                                 r=["hcc", "hlf"], w=["hcc"])
                            ccsrc, cck = CC, "hcc"
                        dec = DEC[b2]
                        P.op("act", lambda e, dec=dec, tot=tot, nch=nch: e.activation(out=dec[:, 0:nch, :], in_=tot, func=AF.Exp),
                             r=["hcum"], w=[("hdec", b2)])
                        P.op("act", lambda e, Tt=Tt, ccsrc=ccsrc: e.activation(out=EK[:, 0:Tt], in_=ccsrc[:, 0:Tt], func=AF.Exp, scale=-1.0),
                             r=[cck], w=["hek"])
                        ktl = KTL[b2]
                        P.op("dve", lambda e, Tt=Tt: e.tensor_tensor(out=KP[:, 0:Tt], in0=KK[:, 0:Tt], in1=EK[:, 0:Tt], op=ALU.mult),
                             r=["hk", "hek"], w=["hkp"])
                        P.op("act", lambda e, Tt=Tt, ktl=ktl: e.activation(out=ktl[:, 0:Tt], in_=KP[:, 0:Tt], func=AF.Copy),
                             r=["hkp"], w=[("hkt", b2)])
                        P.op("dve", lambda e, Tt=Tt, dec=dec, nch=nch: e.tensor_tensor(
                            out=KPB[:, 0:Tt].rearrange("p (c l) -> p c l", l=32), in0=KP[:, 0:Tt].rearrange("p (c l) -> p c l", l=32),
                            in1=dec[:, 0:nch, :].to_broadcast([128, nch, 32]), op=ALU.mult),
                            r=["hkp", ("hdec", b2)], w=["hkpb"])
                        kpt = KPT[b2]
                        for cb in range(nch // 4):
                            for cc_ in range(4):
                                ch = cb * 4 + cc_
                                P.op("pe", lambda e, ch=ch, cc_=cc_: e.matmul(psb7[0:32, cc_ * 128:(cc_ + 1) * 128], KPB[:, ch * 32:(ch + 1) * 32], identb[:], start=True, stop=True),
                                    r=["hkpb", "identb"], w=["psb7"])
                            P.op("act", lambda e, cb=cb, kpt=kpt: e.activation(
                                out=kpt[:, cb * 4:(cb + 1) * 4, :],
                                in_=psb7[0:32, 0:512].rearrange("p (c d) -> p c d", d=128), func=AF.Copy),
                                r=["psb7"], w=[("hkpt", b2)])
                        if lat:
                            qp = QP[b2]
                            P.op("act", lambda e, Tt=Tt, ccsrc=ccsrc: e.activation(out=EQ[:, 0:Tt], in_=ccsrc[:, 0:Tt], func=AF.Exp),
                                 r=[cck], w=["heq"])
                            P.op("dve", lambda e, Tt=Tt, c0=c0, qp=qp: e.tensor_tensor(out=qp[:, 0:Tt], in0=QS[:, c0:c0 + Tt], in1=EQ[:, 0:Tt], op=ALU.mult),
                                 r=["QS", "heq"], w=[("hqp", b2)])
                        nblk = Tt // 128
                        blks = range(nblk) if d == 0 else range(nblk - 1, -1, -1)
                        for bi_, tb in enumerate(blks):
                            kb = c0 // 128 + tb
                            bsl = slice(tb * 128, (tb + 1) * 128)
                            if lat:
                                am = AM[bi_ % 2]
                                P.op("pe", lambda e, ktl=ktl, qp=qp, bsl=bsl: e.matmul(ps[0][:, 0:128], ktl[:, bsl], qp[:, bsl], start=True, stop=True),
                                     r=[("hkt", b2), ("hqp", b2)], w=[PSK[0]])
                                P.op("dve", lambda e, am=am, mask=mask: e.tensor_tensor(out=am[:], in0=ps[0][:, 0:128], in1=mask[:], op=ALU.mult),
                                     r=[PSK[0], mk], w=[("ham", bi_ % 2)])
                                P.op("pe", lambda e, am=am, kb=kb: e.matmul(ps[1][:, 0:128], VTK[:, kb, :], am[:], start=True, stop=False),
                                     r=["VTK", ("ham", bi_ % 2)], w=[PSK[1]])
                            chs = range(4) if d == 0 else range(3, -1, -1)
                            for ci_, cc_ in enumerate(chs):
                                ch = tb * 4 + cc_
                                gch = kb * 4 + cc_
                                if lat:
                                    P.op("pe", lambda e, qp=qp, tb=tb, cc_=cc_, ci_=ci_: e.matmul(
                                        ps[1][:, cc_ * 32:(cc_ + 1) * 32], SBF[:], qp[:, tb * 128 + cc_ * 32:tb * 128 + (cc_ + 1) * 32],
                                        start=False, stop=(ci_ == 3)),
                                        r=["hSbf", ("hqp", b2)], w=[PSK[1]])
                                P.op("pe", lambda e, kpt=kpt, ch=ch, gch=gch: e.matmul(ps[2][:, 0:128], kpt[:, ch, :], VCH[:, gch, :], start=True, stop=True),
                                     r=[("hkpt", b2), "VCH"], w=[PSK[2]])
                                P.op("dve", lambda e, dec=dec, ch=ch: e.scalar_tensor_tensor(out=S32[:], in0=S32[:], scalar=dec[:, ch, :], in1=ps[2][:, 0:128],
                                                                                         op0=ALU.mult, op1=ALU.add),
                                     r=["hS32", ("hdec", b2), PSK[2]], w=["hS32"])
                                P.op("act", lambda e: e.activation(out=SBF[:], in_=S32[:], func=AF.Copy), r=["hS32"], w=["hSbf"])
                            if lat:
                                osl = slice(c0 + tb * 128, c0 + (tb + 1) * 128)
                                if d == 0:
                                    P.op("act", lambda e, osl=osl: e.activation(out=OACC[:, osl], in_=ps[1][:, 0:128], func=AF.Copy),
                                         r=[PSK[1]], w=["OACC"])
                                else:
                                    P.op("dve", lambda e, osl=osl: e.tensor_tensor(out=OACC[:, osl], in0=OACC[:, osl], in1=ps[1][:, 0:128], op=ALU.add),
                                         r=[PSK[1], "OACC"], w=["OACC"])
                for qi, (c0, Tt) in enumerate(tiles):
                    raw, rk = hload(OFF["gb"] + h * 128, c0, Tt)
                    P.op("act", lambda e, raw=raw, Tt=Tt: e.activation(out=GB[:, 0:Tt], in_=raw[:, 0:Tt], func=AF.Silu), r=[rk], w=["hgb"])
                    P.op("act", lambda e, c0=c0, Tt=Tt: e.activation(out=Y1[:, 0:Tt], in_=OACC[:, c0:c0 + Tt], func=AF.Square), r=["OACC"], w=["hy1"])
                    split2_mm(ps[4][:, 0:Tt], [PSK[4]], onesb[:], "onesb", Y1[:, 0:Tt], "hy1", Tt,
                              SQH[0][:, 0:Tt], ("sqh", 0), SQL[0][:, 0:Tt], ("sql", 0), True, True)
                    P.op("act", lambda e, Tt=Tt: e.activation(out=Y1[:, 0:Tt], in_=ps[4][:, 0:Tt], func=AF.Sqrt, scale=1.0 / HD, bias=epsc[:, 0:1]),
                         r=[PSK[4], "epsc"], w=["hy1"])
                    P.op("dve", lambda e, Tt=Tt: e.reciprocal(out=Y1[:, 0:Tt], in_=Y1[:, 0:Tt]), r=["hy1"], w=["hy1"])
                    P.op("dve", lambda e, c0=c0, Tt=Tt: e.scalar_tensor_tensor(out=Y1[:, 0:Tt], in0=OACC[:, c0:c0 + Tt], scalar=og_[:, 0:1], in1=Y1[:, 0:Tt],
                                                                           op0=ALU.mult, op1=ALU.mult), r=["OACC", "ogain", "hy1"], w=["hy1"])
                    yb = YB[qi % 2]
                    P.op("dve", lambda e, Tt=Tt, yb=yb: e.tensor_tensor(out=yb[:, 0:Tt], in0=Y1[:, 0:Tt], in1=GB[:, 0:Tt], op=ALU.mult),
                         r=["hy1", "hgb"], w=[("hyb", qi % 2)])
                    P.dma("sp", catT[AW + h * 128:AW + (h + 1) * 128, c0:c0 + Tt], yb[:, 0:Tt], r=[("hyb", qi % 2)],
                          w=[("catT", NQ + h, c0)])

        for st in _phase(STOPI >= _ORD.index('D')):
            P.barrier()
            WB = [sb(st, "wbd%d" % i, [128, KT, 512], BF16) for i in range(NWB)]
            H = sb(st, "Hd", [128, DC, TT], BF16)
            A = sb(st, "Ad", [128, FC, TT], BF16)
            XC = [sb(st, "xcd%d" % i, [128, TT]) for i in range(4)]
            XO = [sb(st, "xod%d" % i, [128, TT]) for i in range(4)]
            SQ = [sb(st, "sqd%d" % i, [128, TT]) for i in range(2)]
            TMP = [sb(st, "tmpd%d" % i, [128, TT]) for i in range(2)]
            SG = [sb(st, "sgd%d" % i, [128, TT]) for i in range(2)]
            wci = conv_w_in[0]
            for (c0, Tt) in tiles:
                P.dma("sp", H[:, :, 0:Tt], catT[:, c0:c0 + Tt].rearrange("(c p) t -> p c t", p=128),
                      r=[("catT", hh, c0) for hh in range(DC)], w=[("H", n) for n in range(DC)])
                gemm(WB, H, lambda k: ("H", k), DC, Tt, colblocks(ab_w_out[0], 0, D),
                     resid_epilogue(c0, Tt, lambda n: RS[:, 0, 1, n, 0:1], XC, XO, SQ))
                ffn(WB, c0, Tt, 0, 1, 0, H, A, XC, XO, SQ, TMP, SG)
                ffn(WB, c0, Tt, 1, 0, 0, H, A, XC, XO, SQ, TMP, SG)
                adaln(c0, Tt, 1, 1, 0, H, XC, TMP)
                def epib(bi, j, pap, pk, c0=c0, Tt=Tt):
                    n = bi * 4 + j
                    xo = XO[n % 4]
                    P.op("act", lambda e: e.activation(out=xo[:, 0:Tt], in_=pap, func=AF.Copy), r=[pk], w=[("xo", n % 4)])
                    P.dma("sp", cbT[n * 128:(n + 1) * 128, c0:c0 + Tt], xo[:, 0:Tt], r=[("xo", n % 4)], w=[("cbT", n, c0)])
                gemm(WB, H, lambda k: ("H", k), DC, Tt, colblocks(wci, 0, D), epib)
                blocks = [[wci[:, D + j0 * 128:D + j0 * 128 + 256], wci[:, 2 * D + j0 * 128:2 * D + j0 * 128 + 256]] for j0 in range(0, DC, 2)]

                def epiu(bi, j, pap, pk, c0=c0, Tt=Tt):
                    jj = j % 2
                    if j < 2:
                        P.op("act", lambda e: e.activation(out=SG[jj][:, 0:Tt], in_=pap, func=AF.Copy), r=[pk], w=[("sg", jj)])
                    else:
                        n = bi * 2 + jj
                        xo = XO[n % 4]
                        P.op("dve", lambda e: e.tensor_tensor(out=xo[:, 0:Tt], in0=SG[jj][:, 0:Tt], in1=pap, op=ALU.mult),
                             r=[pk, ("sg", jj)], w=[("xo", n % 4)])
                        P.dma("sp", cuT[n * 128:(n + 1) * 128, c0:c0 + Tt], xo[:, 0:Tt], r=[("xo", n % 4)], w=[("cuT", n, c0)])
                gemm(WB, H, lambda k: ("H", k), DC, Tt, blocks, epiu)

            UH = [sb(st, "uh%d" % i, [128, TT + 2]) for i in range(2)]
            BG = [sb(st, "bg%d" % i, [128, TT]) for i in range(2)]
            for (c0, Tt) in tiles:
                for n in range(DC):
                    uh = UH[n % 2]
                    bg = BG[n % 2]
                    lo = max(c0 - 1, 0)
                    hi = min(c0 + Tt + 1, T)
                    if c0 == 0:
                        P.op("dve", lambda e, uh=uh: e.memset(uh[:, 0:1], 0.0), w=[("uh", n % 2)])
                    if c0 + Tt == T:
                        P.op("dve", lambda e, uh=uh, Tt=Tt: e.memset(uh[:, Tt + 1:Tt + 2], 0.0), w=[("uh", n % 2)])
                    P.dma("sp", uh[:, lo - (c0 - 1):hi - (c0 - 1)], cuT[n * 128:(n + 1) * 128, lo:hi],
                          r=[("cuT", n, cc0) for (cc0, _t) in tiles], w=[("uh", n % 2)])
                    P.dma("sp", bg[:, 0:Tt], cbT[n * 128:(n + 1) * 128, c0:c0 + Tt], r=[("cbT", n, c0)], w=[("bg", n % 2)])
                    t = TMP[n % 2]
                    P.op("dve", lambda e, uh=uh, t=t, n=n, Tt=Tt: e.tensor_scalar(out=t[:, 0:Tt], in0=uh[:, 0:Tt], scalar1=cwT[:, 0, n:n + 1], scalar2=None,
                                                                               op0=ALU.mult), r=[("uh", n % 2), "cwT"], w=[("tmp", n % 2)])
                    P.op("dve", lambda e, uh=uh, t=t, n=n, Tt=Tt: e.scalar_tensor_tensor(out=t[:, 0:Tt], in0=uh[:, 1:Tt + 1], scalar=cwT[:, 1, n:n + 1],
                                                                                      in1=t[:, 0:Tt], op0=ALU.mult, op1=ALU.add),
                         r=[("uh", n % 2), "cwT", ("tmp", n % 2)], w=[("tmp", n % 2)])
                    P.op("dve", lambda e, uh=uh, t=t, n=n, Tt=Tt: e.scalar_tensor_tensor(out=t[:, 0:Tt], in0=uh[:, 2:Tt + 2], scalar=cwT[:, 2, n:n + 1],
                                                                                      in1=t[:, 0:Tt], op0=ALU.mult, op1=ALU.add),
                         r=[("uh", n % 2), "cwT", ("tmp", n % 2)], w=[("tmp", n % 2)])
                    P.op("dve", lambda e, bg=bg, t=t, n=n, Tt=Tt: e.tensor_tensor(out=H[:, n, 0:Tt], in0=t[:, 0:Tt], in1=bg[:, 0:Tt], op=ALU.mult),
                         r=[("tmp", n % 2), ("bg", n % 2)], w=[("H", n)])
                gemm(WB, H, lambda k: ("H", k), DC, Tt, colblocks(conv_w_out[0], 0, D),
                     resid_epilogue(c0, Tt, lambda n: RS[:, 1, 1, n, 0:1], XC, XO, SQ))
                ffn(WB, c0, Tt, 1, 1, 0, H, A, XC, XO, SQ, TMP, SG)

        for st in _phase(STOPI >= _ORD.index('D')):
            P.barrier()
            XCF = [sb(st, "xcf%d" % i, [128, TT]) for i in range(2)]
            FR1 = sb(st, "fr1", [128, TT])
            FS = [sb(st, "fs%d" % i, [128, TT], BF16) for i in range(3)]
            OUTB = [sb(st, "outb%d" % i, [128, 4, 512]) for i in range(2)]
            gi = 0
            for (c0, Tt) in tiles:
                nb = Tt // 128
                for n in range(DC):
                    sl = n % 4
                    if sl == 0:
                        oset = gi % 2
                        gi += 1
                    xc = XCF[n % 2]
                    xk = ("xcf", n % 2)
                    bank, bkey = TBK[n % 2]
                    P.dma("sp", xc[:, 0:Tt], xT[n * 128:(n + 1) * 128, c0:c0 + Tt], r=[("xT", n, c0)], w=[xk])
                    P.op("act", lambda e: e.activation(out=FS[0][:, 0:Tt], in_=xc[:, 0:Tt], func=AF.Copy), r=[xk], w=["fs0"])
                    P.op("dve", lambda e: e.tensor_tensor(out=FR1[:, 0:Tt], in0=xc[:, 0:Tt], in1=FS[0][:, 0:Tt], op=ALU.subtract),
                         r=[xk, "fs0"], w=["fr1"])
                    P.op("act", lambda e: e.activation(out=FS[1][:, 0:Tt], in_=FR1[:, 0:Tt], func=AF.Copy), r=["fr1"], w=["fs1"])
                    P.op("dve", lambda e: e.tensor_tensor(out=FS[2][:, 0:Tt], in0=FR1[:, 0:Tt], in1=FS[1][:, 0:Tt], op=ALU.subtract),
                         r=["fr1", "fs1"], w=["fs2"])
                    for tb in range(nb):
                        for q3 in range(3):
                            P.op("pe", lambda e: e.matmul(bank[:, tb * 128:(tb + 1) * 128], FS[q3][:, tb * 128:(tb + 1) * 128], identb[:],
                                                          start=(q3 == 0), stop=(q3 == 2)),
                                 r=["fs%d" % q3, "identb"], w=[bkey])
                    ob = OUTB[oset]
                    obk = ("outb", oset)
                    P.op("act", lambda e: e.activation(out=ob[:, 0:nb, sl * 128:(sl + 1) * 128],
                                                       in_=bank[:, 0:nb * 128].rearrange("p (b f) -> p b f", f=128), func=AF.Copy),
                         r=[bkey], w=[obk])
                    if sl == 3:
                        for tb in range(nb):
                            P.dma("sp", out[c0 + tb * 128:c0 + (tb + 1) * 128, (n - 3) * 128:(n + 1) * 128], ob[:, tb, :], r=[obk])

        P.emit()
    return nc


def host_consts(cfg):
    T, GW = cfg["T"], cfg["GRID_W"]
    t = np.arange(T)
    pos = np.stack([t // GW, t % GW]).astype(np.float32)
    inv = (10000.0 ** (-np.arange(0, 64, 2, dtype=np.float32) / 64.0)).astype(np.float32)
    p = np.arange(128)
    a = p // 64
    half = (p % 64) // 32
    i = p % 32
    ang = pos[a, :] * inv[i][:, None]
    ropec = np.cos(ang).astype(np.float32)
    sgn = np.where(half == 0, -1.0, 1.0).astype(np.float32)[:, None]
    ropes = (np.sin(ang) * sgn).astype(np.float32)
    pair = np.where(half == 0, p + 32, p - 32)
    perm = np.zeros((128, 128), np.float32)
    perm[pair, p] = 1.0
    ident = np.eye(128, dtype=np.float32)
    s = p[:, None]
    tt = p[None, :]
    same = (s // 32) == (tt // 32)
    maskf = (same & (s <= tt)).astype(np.float32)
    maskb = (same & (s >= tt)).astype(np.float32)
    return dict(k_ropec=ropec, k_ropes=ropes, k_perm=perm.astype(ml_dtypes.bfloat16), k_ident=ident,
                k_identb=ident.astype(ml_dtypes.bfloat16), k_maskf=maskf, k_maskb=maskb)


_NC_CACHE = {}


def run(cfg, inputs, dbg=()):
    key = (tuple(sorted(cfg.items())), tuple(dbg))
    if key not in _NC_CACHE:
        _NC_CACHE[key] = build_nc(cfg, dbg)
    nc = _NC_CACHE[key]
    B = inputs["x"].shape[0]
    consts = host_consts(cfg)
    shared = {k: np.ascontiguousarray(v) for k, v in inputs.items() if k not in ("x", "c", "ctx", "c_ctx")}
    shared.update(consts)
    in_maps = []
    for b in range(B):
        m = dict(shared)
        m["x"] = np.ascontiguousarray(inputs["x"][b])
        m["ctx"] = np.ascontiguousarray(inputs["ctx"][b])
        m["cvec"] = np.ascontiguousarray(np.stack([inputs["c"][b], inputs["c_ctx"]]))
        in_maps.append(m)
    res = run_bass_kernel_spmd(nc, in_maps, core_ids=list(range(B)))
    return res


def kernel(**inputs):
    inputs = {k: np.asarray(v) for k, v in inputs.items()}
    res = run(CFG, inputs)
    return np.stack([r["out"] for r in res.results]).astype(np.float32)
```

```python
import contextlib
import numpy as np
import ml_dtypes
import concourse.bass as bass
import concourse.mybir as mybir
from concourse.bass_utils import run_bass_kernel_spmd

F32 = mybir.dt.float32
BF16 = mybir.dt.bfloat16
AF = mybir.ActivationFunctionType
ALU = mybir.AluOpType

ENGS = ("pe", "act", "dve", "pool", "sp")
DMA_NSEM = 8


class Op:
    __slots__ = ("eng", "fn", "deps", "is_dma", "idx", "has_dep", "cnt", "qi")

    def __init__(self, eng, fn, is_dma):
        self.eng = eng
        self.fn = fn
        self.is_dma = is_dma
        self.deps = ()
        self.has_dep = False
        self.cnt = None
        self.qi = None


def _phase(cond):
    if cond:
        with contextlib.ExitStack() as st:
            yield st


class _Rec:
    def __getattr__(self, name):
        return lambda *a, **k: (name, a, k)


_REC = _Rec()


def _replay(call):
    name, a, k = call
    return lambda e: getattr(e, name)(*a, **k)


class Prog:
    def __init__(self, nc):
        self.nc = nc
        self.ops = {e: [] for e in ENGS}
        self.lastw = {}
        self.readers = {}

    def _add(self, eng, fn, r, w, is_dma=False):
        op = Op(eng, fn, is_dma)
        deps = set()
        for k in r:
            lw = self.lastw.get(k)
            if lw is not None:
                deps.add(lw)
        for k in w:
            lw = self.lastw.get(k)
            if lw is not None:
                deps.add(lw)
            rd = self.readers.get(k)
            if rd:
                deps.update(rd.values())
        op.deps = tuple(deps)
        for d in op.deps:
            d.has_dep = True
        for k in w:
            self.lastw[k] = op
            self.readers[k] = {}
        rk = id(op) if is_dma else eng
        for k in r:
            if k not in w:
                self.readers.setdefault(k, {})[rk] = op
        op.idx = len(self.ops[eng])
        self.ops[eng].append(op)
        return op

    def op(self, eng, fn, r=(), w=()):
        return self._add(eng, _replay(fn(_REC)), r, w, False)

    def dma(self, q, out, in_, r=(), w=(), **kw):
        return self._add(q, lambda e: e.dma_start(out=out, in_=in_, **kw), r, w, True)

    def barrier(self):
        lasts = []
        for e in ENGS:
            ops = self.ops[e]
            for o in reversed(ops):
                if not o.is_dma and o.fn is not None:
                    lasts.append(o)
                    break
            cnt = 0
            for o in reversed(ops):
                if o.is_dma:
                    lasts.append(o)
                    cnt += 1
                    if cnt >= DMA_NSEM:
                        break
        for e in ENGS:
            op = Op(e, None, False)
            op.deps = tuple(lasts)
            for d in op.deps:
                d.has_dep = True
            op.idx = len(self.ops[e])
            self.ops[e].append(op)

    def emit(self):
        nc = self.nc
        with contextlib.ExitStack() as st:
            esem = {e: st.enter_context(nc.semaphore("s_" + e)) for e in ENGS}
            dsem = {e: [st.enter_context(nc.semaphore("d_%s%d" % (e, i))) for i in range(DMA_NSEM)]
                    for e in ("sp", "pool", "act")}
            for e in ENGS:
                c = 0
                q = 0
                for op in self.ops[e]:
                    if op.is_dma:
                        op.qi = q
                        q += 1
                    elif op.has_dep:
                        c += 1
                        op.cnt = c
            block = st.enter_context(nc.Block())

            def run(ename, eng):
                known = {}

                def wait(sem, val, key):
                    if known.get(key, 0) >= val:
                        return
                    known[key] = val
                    eng.wait_ge(sem, val)

                for op in self.ops[ename]:
                    for d in op.deps:
                        if d.is_dma:
                            wait(dsem[d.eng][d.qi % DMA_NSEM], 16 * (d.qi // DMA_NSEM + 1),
                                 ("d", d.eng, d.qi % DMA_NSEM))
                        else:
                            if d.eng == "pe" and ename == "pe":
                                continue
                            wait(esem[d.eng], d.cnt, ("e", d.eng))
                    if op.fn is None:
                        continue
                    if op.is_dma:
                        i = op.qi
                        if i >= DMA_NSEM:
                            wait(dsem[ename][i % DMA_NSEM], 16 * (i // DMA_NSEM),
                                 ("d", ename, i % DMA_NSEM))
                        op.fn(eng).then_inc(dsem[ename][i % DMA_NSEM], 16)
                    else:
                        ins = op.fn(eng)
                        if op.cnt is not None:
                            ins.then_inc(esem[ename], 1)
                nq = sum(1 for o in self.ops[ename] if o.is_dma)
                for s_ in range(min(nq, DMA_NSEM)):
                    last = ((nq - 1 - s_) // DMA_NSEM) * DMA_NSEM + s_
                    wait(dsem[ename][s_], 16 * (last // DMA_NSEM + 1), ("d", ename, s_))

            block.tensor(lambda eng: run("pe", eng))
            block.scalar(lambda eng: run("act", eng))
            block.vector(lambda eng: run("dve", eng))
            block.gpsimd(lambda eng: run("pool", eng))
            block.sync(lambda eng: run("sp", eng))


CFG = dict(D=4096, T=2048, TC=256, GRID_W=64, TT=512)
EPS = 1e-6
HD = 128
KT = 16
NWB = 3


def derive(cfg):
    c = dict(cfg)
    D = c["D"]
    c["DC"] = D // 128
    c["AW"] = D // 2
    c["NQ"] = c["AW"] // HD
    c["NKV"] = c["NQ"] // 4
    c["KVW"] = c["NKV"] * HD
    c["HW"] = D - c["AW"]
    c["HH"] = c["HW"] // HD
    c["F"] = 2 * D
    c["S"] = c["T"] + c["TC"]
    aw, kvw, hw = c["AW"], c["KVW"], c["HW"]
    offs = np.cumsum([0, aw, kvw, kvw, hw, hw, hw, hw, hw])
    c["OFF"] = dict(zip(["qa", "ka", "va", "qb", "zf", "zb", "ib", "gb", "end"], [int(o) for o in offs]))
    c["ABIN"] = int(offs[-1])
    return c


def build_nc(cfg, dbg=()):
    c = derive(cfg)
    _ORD = 'LMABCDZ'
    STOPI = _ORD.index(cfg.get('STOP', 'Z'))
    D, T, TC, S, DC, F, TT = c["D"], c["T"], c["TC"], c["S"], c["DC"], c["F"], c["TT"]
    NQ, NKV, HH, AW, HW, ABIN, OFF = c["NQ"], c["NKV"], c["HH"], c["AW"], c["HW"], c["ABIN"], c["OFF"]
    FC = F // 128
    nc = bass.Bass("TRN2", target_bir_lowering=False)
    P = Prog(nc)

    def din(name, shape, dt=F32):
        return nc.dram_tensor(name, list(shape), dt, kind="ExternalInput").ap()

    x_in = din("x", [T, D]); ctx_in = din("ctx", [TC, D]); cvec = din("cvec", [2, D])
    mod_w = din("mod_w", [2, D, 9 * D]); mod_b = din("mod_b", [2, 9 * D]); norm_g = din("norm_g", [2, 3, D])
    ffn_w1 = din("ffn_w1", [2, 2, D, 2 * F]); ffn_w2 = din("ffn_w2", [2, 2, F, D])
    ab_w_in = din("ab_w_in", [1, D, ABIN]); ab_w_out = din("ab_w_out", [1, D, D])
    q_gain = din("attn_q_gain", [1, HD]); k_gain = din("attn_k_gain", [1, HD])
    lb_logits = din("hgrn_lb_logits", [2, 3, HW]); out_gain = din("hgrn_out_gain", [1, HD])
    conv_w_in = din("conv_w_in", [1, D, 3 * D]); conv_w = din("conv_w", [1, 3, D]); conv_w_out = din("conv_w_out", [1, D, D])
    k_ropec = din("k_ropec", [128, T]); k_ropes = din("k_ropes", [128, T])
    k_perm = din("k_perm", [128, 128], BF16); k_ident = din("k_ident", [128, 128])
    k_identb = din("k_identb", [128, 128], BF16)
    k_maskf = din("k_maskf", [128, 128]); k_maskb = din("k_maskb", [128, 128])
    out = nc.dram_tensor("out", [T, D], F32, kind="ExternalOutput").ap()

    def dscr(name, shape, dt=F32):
        kind = "ExternalOutput" if name in dbg else "Internal"
        return nc.dram_tensor(name, list(shape), dt, kind=kind).ap()

    xT = dscr("xT", [D, S])
    projT = dscr("projT", [ABIN, S])
    catT = dscr("catT", [D, T], BF16)
    cbT = dscr("cbT", [D, T])
    cuT = dscr("cuT", [D, T])

    tiles = [(i * TT, TT) for i in range(T // TT)]
    ctile = (T, TC)

    with contextlib.ExitStack() as top:
        def sb(st, name, shape, dt=F32):
            return st.enter_context(nc.sbuf_tensor(name, list(shape), dt))

        ones32 = sb(top, "ones32", [128, 128]); onesb = sb(top, "onesb", [128, 128], BF16)
        ident = sb(top, "ident", [128, 128]); identb = sb(top, "identb", [128, 128], BF16)
        epsc = sb(top, "epsc", [128, 1])
        modT = sb(top, "modT", [128, 2, 9 * DC, 2])
        modbT = sb(top, "modbT", [128, 2, 9 * DC])
        ngT = sb(top, "ngT", [128, 6, DC, 1])
        GS = sb(top, "GS", [128, 2, 3, 2, DC, 2])
        RS = sb(top, "RS", [128, 2, 3, DC, 2])
        rstd = sb(top, "rstd", [128, TT])
        SQH = [sb(top, "sqh%d" % i, [128, TT], BF16) for i in range(2)]
        SQL = [sb(top, "sql%d" % i, [128, TT], BF16) for i in range(2)]
        qg_ = sb(top, "qgain", [128, 1]); kg_ = sb(top, "kgain", [128, 1]); og_ = sb(top, "ogain", [128, 1])
        LB = sb(top, "LB", [128, 2, 3, HH]); cwT = sb(top, "cwT", [128, 3, DC])
        craw = sb(top, "craw", [128, 2, DC])
        ps = [top.enter_context(nc.psum_tensor("ps%d" % i, [128, 512], F32)) for i in range(8)]
        psb6 = ps[6]
        psb7 = ps[7]
        PSK = [("ps", i) for i in range(6)]
        PS5 = [("ps", 5)]
        TBK = [(ps[5], ("ps", 5)), (ps[6], "psb6")]

        P.op("dve", lambda e: e.memset(ones32[:], 1.0), w=["ones32"])
        P.op("dve", lambda e: e.memset(onesb[:], 1.0), w=["onesb"])
        P.op("dve", lambda e: e.memset(epsc[:], EPS), w=["epsc"])
        P.dma("sp", ident[:], k_ident, w=["ident"])
        P.dma("sp", identb[:], k_identb, w=["identb"])
        rowt = [sb(top, "rowt%d" % i, [128, 128]) for i in range(2)]
        rtc = [0]

        def load_rows_T(dst2d, src, nrows, key):
            for r0 in range(0, nrows, 128):
                rows = min(128, nrows - r0)
                i = rtc[0] % 2
                rtc[0] += 1
                if rows < 128:
                    P.op("dve", lambda e: e.memset(rowt[i][:], 0.0), w=[("rowt", i)])
                P.dma("sp", rowt[i][0:rows, :], src[r0:r0 + rows, :], w=[("rowt", i)])
                P.op("pe", lambda e: e.matmul(ps[5][:, 0:128], rowt[i][:], ident[:], start=True, stop=True),
                     r=[("rowt", i), "ident"], w=PS5)
                P.op("act", lambda e: e.activation(out=dst2d[:, r0:r0 + rows], in_=ps[5][:, 0:rows], func=AF.Copy),
                     r=PS5, w=[key])

        import os as _os
        load_rows_T(modbT[:].rearrange("p l n -> p (l n)"), mod_b.rearrange("l (n p) -> (l n) p", p=128), 2 * 9 * DC, "modbT")
        load_rows_T(ngT[:].rearrange("p a c o -> p (a c o)"), norm_g.rearrange("l j (c p) -> (l j c) p", p=128), 6 * DC, "ngT")
        load_rows_T(craw[:].rearrange("p m c -> p (m c)"), cvec.rearrange("m (c p) -> (m c) p", p=128), 2 * DC, "craw")
        load_rows_T(qg_[:], q_gain, 1, "qgain")
        load_rows_T(kg_[:], k_gain, 1, "kgain")
        load_rows_T(og_[:], out_gain, 1, "ogain")
        load_rows_T(LB[:].rearrange("p d l h -> p (d l h)"), lb_logits.rearrange("d l (h p) -> (d l h) p", p=128), 6 * HH, "LB")
        load_rows_T(cwT[:].rearrange("p j c -> p (j c)"), conv_w.rearrange("o j (c p) -> (o j c) p", p=128), 3 * DC, "cwT")

        def split2_mm(bank_ap, bank_keys, lhsT_b, lkey, src, srckey, n_, hi, hik, lo, lok, start, stop):
            P.op("dve", lambda e: e.tensor_copy(out=hi, in_=src), r=[srckey], w=[hik])
            P.op("dve", lambda e: e.tensor_tensor(out=lo, in0=src, in1=hi, op=ALU.subtract), r=[srckey, hik], w=[lok])
            P.op("pe", lambda e: e.matmul(bank_ap, lhsT_b, hi, start=start, stop=False), r=[hik, lkey], w=bank_keys)
            P.op("pe", lambda e: e.matmul(bank_ap, lhsT_b, lo, start=False, stop=stop), r=[lok, lkey], w=bank_keys)

        wctr = [0]

        def gemm(st_wb, X, xkey, KC, Tt, blocks, epi):
            WB = st_wb
            nkt = (KC + KT - 1) // KT
            sched = [(bi, kt) for bi in range(len(blocks)) for kt in range(nkt)]

            def load(i):
                bi, kt = sched[i]
                slot = (wctr[0] + i) % NWB
                k0 = kt * KT
                kn = min(KT, KC - k0)
                co = 0
                for si, seg in enumerate(blocks[bi]):
                    wdt = seg.shape[1]
                    P.dma("pool", WB[slot][:, 0:kn, co:co + wdt],
                          seg[k0 * 128:(k0 + kn) * 128, :].rearrange("(c p) n -> p c n", p=128),
                          w=[("wb", slot, si)])
                    co += wdt

            PF = NWB - 1
            for i in range(min(PF, len(sched))):
                load(i)
            for i, (bi, kt) in enumerate(sched):
                if i + PF < len(sched):
                    load(i + PF)
                slot = (wctr[0] + i) % NWB
                k0 = kt * KT
                kn = min(KT, KC - k0)
                ncols = sum(seg.shape[1] for seg in blocks[bi])
                wkeys = [("wb", slot, si) for si in range(len(blocks[bi]))]
                nch = ncols // 128
                for j in range(nch):
                    for k in range(kn):
                        kk = k0 + k
                        P.op("pe", lambda e, j=j, k=k, kk=kk, slot=slot: e.matmul(
                            ps[j][:, 0:Tt], WB[slot][:, k, j * 128:(j + 1) * 128], X[:, kk, 0:Tt],
                            start=(kk == 0), stop=(kk == KC - 1)),
                            r=[xkey(kk)] + wkeys, w=[PSK[j]])
                if kt == nkt - 1:
                    for j in range(nch):
                        epi(bi, j, ps[j][:, 0:Tt], PSK[j])
            wctr[0] += len(sched)

        def colblocks(Wap, c0, c1, width=512):
            return [[Wap[:, a:min(a + width, c1)]] for a in range(c0, c1, width)]

        def stats_tail(n, nlast, src, srckey, Tt, SQ):
            sq = SQ[n % 2]
            P.op("act", lambda e: e.activation(out=sq[:, 0:Tt], in_=src, func=AF.Square),
                 r=[srckey], w=[("sq", n % 2)])
            split2_mm(ps[4][:, 0:Tt], [PSK[4]], onesb[:], "onesb", sq[:, 0:Tt], ("sq", n % 2), Tt,
                      SQH[n % 2][:, 0:Tt], ("sqh", n % 2), SQL[n % 2][:, 0:Tt], ("sql", n % 2), n == 0, n == nlast)
            if n == nlast:
                P.op("act", lambda e: e.activation(out=rstd[:, 0:Tt], in_=ps[4][:, 0:Tt], func=AF.Sqrt,
                                                   scale=1.0 / D, bias=epsc[:, 0:1]),
                     r=[PSK[4], "epsc"], w=["rstd"])
                P.op("dve", lambda e: e.reciprocal(out=rstd[:, 0:Tt], in_=rstd[:, 0:Tt]), r=["rstd"], w=["rstd"])

        def resid_epilogue(c0, Tt, svec, XC, XO, SQ, nchunks=None):
            nchunks = DC if nchunks is None else nchunks

            def ldx(n):
                P.dma("sp", XC[n % 4][:, 0:Tt], xT[n * 128:(n + 1) * 128, c0:c0 + Tt],
                      r=[("xT", n, c0)], w=[("xc", n % 4)])

            def epi(bi, j, pap, pk):
                n = bi * 4 + j
                if n == 0:
                    for m in range(min(4, nchunks)):
                        ldx(m)
                xo = XO[n % 4]
                P.op("dve", lambda e: e.scalar_tensor_tensor(out=xo[:, 0:Tt], in0=pap, scalar=svec(n),
                                                             in1=XC[n % 4][:, 0:Tt], op0=ALU.mult, op1=ALU.add),
                     r=[pk, ("xc", n % 4), "RS"], w=[("xo", n % 4)])
                P.dma("sp", xT[n * 128:(n + 1) * 128, c0:c0 + Tt], xo[:, 0:Tt], r=[("xo", n % 4)], w=[("xT", n, c0)])
                stats_tail(n, nchunks - 1, xo[:, 0:Tt], ("xo", n % 4), Tt, SQ)
                if n + 4 < nchunks:
                    ldx(n + 4)
            return epi

        def adaln(c0, Tt, l, j, col, H, XC, TMP):
            for n in range(DC):
                P.dma("sp", XC[n % 4][:, 0:Tt], xT[n * 128:(n + 1) * 128, c0:c0 + Tt],
                      r=[("xT", n, c0)], w=[("xc", n % 4)])
                t = TMP[n % 2]
                P.op("dve", lambda e, n=n, t=t: e.tensor_tensor(out=t[:, 0:Tt], in0=XC[n % 4][:, 0:Tt], in1=rstd[:, 0:Tt],
                                                                op=ALU.mult),
                     r=[("xc", n % 4), "rstd"], w=[("tmp", n % 2)])
                P.op("act", lambda e, n=n, t=t: e.activation(out=H[:, n, 0:Tt], in_=t[:, 0:Tt], func=AF.Identity,
                                                             scale=GS[:, l, j, 0, n, col:col + 1],
                                                             bias=GS[:, l, j, 1, n, col:col + 1]),
                     r=[("tmp", n % 2), "GS"], w=[("H", n)])

        def ffn(WB, c0, Tt, l, i, col, H, A, XC, XO, SQ, TMP, SG):
            adaln(c0, Tt, l, 2 * i, col, H, XC, TMP)
            w1 = ffn_w1[l, i]
            w2 = ffn_w2[l, i]
            blocks = []
            for j0 in range(0, FC, 4):
                blocks.append([w1[:, j0 * 128:(j0 + 4) * 128]])
                blocks.append([w1[:, F + j0 * 128:F + (j0 + 4) * 128]])

            def epi1(bi, j, pap, pk):
                if bi % 2 == 0:
                    P.op("act", lambda e: e.activation(out=SG[j][:, 0:Tt], in_=pap, func=AF.Silu),
                         r=[pk], w=[("sg", j)])
                else:
                    a = (bi // 2) * 4 + j
                    P.op("dve", lambda e: e.tensor_tensor(out=A[:, a, 0:Tt], in0=SG[j][:, 0:Tt], in1=pap, op=ALU.mult),
                         r=[pk, ("sg", j)], w=[("A", a)])
            gemm(WB, H, lambda k: ("H", k), DC, Tt, blocks, epi1)
            gemm(WB, A, lambda k: ("A", k), FC, Tt, colblocks(w2, 0, D),
                 resid_epilogue(c0, Tt, lambda n: RS[:, l, 2 * i, n, col:col + 1], XC, XO, SQ))

        for st in _phase(STOPI >= _ORD.index('M') and not _os.environ.get('NOM')):
            P.barrier()
            WB = [sb(st, "wbm%d" % i, [128, KT, 512], BF16) for i in range(NWB)]
            cs_ = sb(st, "cs", [128, 2, DC], BF16)
            P.op("act", lambda e: e.activation(out=cs_[:], in_=craw[:], func=AF.Silu), r=["craw"], w=["cs"])
            cs = cs_[:].rearrange("p m c -> p c m")
            for l in range(2):
                def epim(bi, j, pap, pk, l=l):
                    n = bi * 4 + j
                    P.op("dve", lambda e: e.tensor_scalar(out=modT[:, l, n, :], in0=pap, scalar1=modbT[:, l, n:n + 1],
                                                          scalar2=None, op0=ALU.add),
                         r=[pk, "modbT"], w=["modT"])
                gemm(WB, cs, lambda k: "cs", DC, 2, colblocks(mod_w[l], 0, 9 * D), epim)
            for l in range(2):
                for j in range(3):
                    sh = modT[:, l, (3 * j) * DC:(3 * j + 1) * DC, :]
                    sc = modT[:, l, (3 * j + 1) * DC:(3 * j + 2) * DC, :]
                    gt = modT[:, l, (3 * j + 2) * DC:(3 * j + 3) * DC, :]
                    P.op("dve", lambda e, l=l, j=j, sc=sc: e.scalar_tensor_tensor(
                        out=GS[:, l, j, 0, :, :], in0=sc, scalar=1.0,
                        in1=ngT[:, l * 3 + j, :, :].to_broadcast([128, DC, 2]), op0=ALU.add, op1=ALU.mult),
                        r=["modT", "ngT"], w=["GS"])
                    P.op("dve", lambda e, l=l, j=j, sh=sh: e.tensor_copy(out=GS[:, l, j, 1, :, :], in_=sh),
                         r=["modT"], w=["GS"])
                    fac = 1.0 if j == 1 else 0.5
                    P.op("dve", lambda e, l=l, j=j, gt=gt, fac=fac: e.tensor_scalar(
                        out=RS[:, l, j, :, :], in0=gt, scalar1=fac, scalar2=None, op0=ALU.mult),
                        r=["modT"], w=["RS"])

        for st in _phase(STOPI >= _ORD.index('A')):
            P.barrier()
            WB = [sb(st, "wb%d" % i, [128, KT, 512], BF16) for i in range(NWB)]
            H = sb(st, "H", [128, DC, TT], BF16)
            A = sb(st, "A", [128, FC, TT], BF16)
            XC = [sb(st, "xc%d" % i, [128, TT]) for i in range(4)]
            XO = [sb(st, "xo%d" % i, [128, TT]) for i in range(4)]
            SQ = [sb(st, "sq%d" % i, [128, TT]) for i in range(2)]
            TMP = [sb(st, "tmp%d" % i, [128, TT]) for i in range(2)]
            SG = [sb(st, "sg%d" % i, [128, TT]) for i in range(4)]
            XIN = [sb(st, "xin%d" % i, [128, 512]) for i in range(2)]
            XR1 = sb(st, "xr1", [128, 512])
            XS = [sb(st, "xs%d" % i, [128, 512], BF16) for i in range(3)]

            def transpose_in(src, r0, c0, Tt):
                nb = Tt // 128
                pc = 0
                for tb in range(nb):
                    for n0 in range(0, DC, 4):
                        xb = XIN[pc % 2]
                        xk = ("xin", pc % 2)
                        bank, bkey = TBK[pc % 2]
                        xo = XO[pc % 4]
                        xok = ("xo", pc % 4)
                        pc += 1
                        P.dma("sp", xb[:], src[r0 + tb * 128:r0 + (tb + 1) * 128, n0 * 128:(n0 + 4) * 128], w=[xk])
                        P.op("act", lambda e: e.activation(out=XS[0][:], in_=xb[:], func=AF.Copy), r=[xk], w=["xs0"])
                        P.op("dve", lambda e: e.tensor_tensor(out=XR1[:], in0=xb[:], in1=XS[0][:], op=ALU.subtract), r=[xk, "xs0"], w=["xr1"])
                        P.op("act", lambda e: e.activation(out=XS[1][:], in_=XR1[:], func=AF.Copy), r=["xr1"], w=["xs1"])
                        P.op("dve", lambda e: e.tensor_tensor(out=XS[2][:], in0=XR1[:], in1=XS[1][:], op=ALU.subtract), r=["xr1", "xs1"], w=["xs2"])
                        for sl in range(4):
                            for q3 in range(3):
                                P.op("pe", lambda e: e.matmul(bank[:, sl * 128:(sl + 1) * 128], XS[q3][:, sl * 128:(sl + 1) * 128], identb[:],
                                                              start=(q3 == 0), stop=(q3 == 2)),
                                     r=["xs%d" % q3, "identb"], w=[bkey])
                        P.op("act", lambda e: e.activation(out=xo[:, 0:512], in_=bank[:, 0:512], func=AF.Copy), r=[bkey], w=[xok])
                        for sl in range(4):
                            n = n0 + sl
                            P.dma("sp", xT[n * 128:(n + 1) * 128, c0 + tb * 128:c0 + (tb + 1) * 128], xo[:, sl * 128:(sl + 1) * 128],
                                  r=[xok], w=[("xT", n, c0)])
                for n in range(0 if not _os.environ.get("A_NOSTAT") else DC, DC):
                    P.dma("sp", XC[n % 4][:, 0:Tt], xT[n * 128:(n + 1) * 128, c0:c0 + Tt],
                          r=[("xT", n, c0)], w=[("xc", n % 4)])
                    stats_tail(n, DC - 1, XC[n % 4][:, 0:Tt], ("xc", n % 4), Tt, SQ)

            win = ab_w_in[0]
            for ti, (c0, Tt) in enumerate(tiles + [ctile]):
                if ti >= int(_os.environ.get("A_TILES", "99")):
                    continue
                is_ctx = ti == len(tiles)
                col = 1 if is_ctx else 0
                ASTOP = int(_os.environ.get("ASTOP", "9"))
                transpose_in(ctx_in if is_ctx else x_in, 0 if is_ctx else c0, c0, Tt)
                if ASTOP < 2:
                    continue
                ffn(WB, c0, Tt, 0, 0, col, H, A, XC, XO, SQ, TMP, SG)
                if ASTOP < 3:
                    continue
                adaln(c0, Tt, 0, 1, col, H, XC, TMP)
                if is_ctx:
                    rng = [(OFF["ka"], OFF["qb"]), (OFF["zf"], OFF["gb"])]
                else:
                    rng = [(0, ABIN)]
                for (a0, a1) in rng:
                    blocks = colblocks(win, a0, a1)

                    def epip(bi, j, pap, pk, a0=a0, c0=c0, Tt=Tt):
                        n = bi * 4 + j
                        xo = XO[n % 4]
                        P.op("act", lambda e: e.activation(out=xo[:, 0:Tt], in_=pap, func=AF.Copy),
                             r=[pk], w=[("xo", n % 4)])
                        P.dma("sp", projT[a0 + n * 128:a0 + (n + 1) * 128, c0:c0 + Tt], xo[:, 0:Tt],
                              r=[("xo", n % 4)], w=[("projT", a0 + n * 128, c0)])
                    gemm(WB, H, lambda k: ("H", k), DC, Tt, blocks, epip)

        def proj_keys(row0):
            return [("projT", row0, c0) for (c0, _t) in tiles + [ctile]]

        SC = float(HD) ** -0.5
        for st in _phase(STOPI >= _ORD.index('B')):
            P.barrier()
            ropec = sb(st, "ropec", [128, T]); ropes = sb(st, "ropes", [128, T]); perm = sb(st, "perm", [128, 128], BF16)
            QGH = sb(st, "aqgh", [128, TT], BF16); QGL = sb(st, "aqgl", [128, TT], BF16)
            P.dma("sp", ropec[:], k_ropec, w=["ropec"]); P.dma("sp", ropes[:], k_ropes, w=["ropes"])
            P.dma("sp", perm[:], k_perm, w=["perm"])
            KR = sb(st, "KR", [128, S], BF16)
            VT = sb(st, "VT", [128, S // 128, 128], BF16)
            QR = [sb(st, "QR%d" % i, [128, TT], BF16) for i in range(2)]
            RAW = [sb(st, "raw%d" % i, [128, TT]) for i in range(2)]
            SQ = sb(st, "asq", [128, TT]); QG = sb(st, "aqg", [128, TT]); RT = sb(st, "art", [128, TT])
            T1 = sb(st, "at1", [128, TT]); T2 = sb(st, "at2", [128, TT])
            VB = sb(st, "avb", [128, TT], BF16)
            PT = [sb(st, "PT%d" % i, [128, TT], BF16) for i in range(3)]
            RSUM = sb(st, "arsum", [128, TT]); OB = [sb(st, "aob%d" % i, [128, TT], BF16) for i in range(2)]
            rawc = [0]

            def load_raw(row0, c0, Tt):
                i = rawc[0] % 2
                rawc[0] += 1
                P.dma("sp", RAW[i][:, 0:Tt], projT[row0:row0 + 128, c0:c0 + Tt], r=[("projT", row0, c0)], w=[("raw", i)])
                return RAW[i], ("raw", i)

            def norm_rope(raw, rk, gain, gk, c0, Tt, rope, dst, dk):
                P.op("act", lambda e: e.activation(out=SQ[:, 0:Tt], in_=raw[:, 0:Tt], func=AF.Square), r=[rk], w=["asq"])
                P.op("dve", lambda e: e.tensor_scalar(out=QG[:, 0:Tt], in0=raw[:, 0:Tt], scalar1=gain[:, 0:1], scalar2=None,
                                                      op0=ALU.mult), r=[rk, gk], w=["aqg"])
                split2_mm(ps[4][:, 0:Tt], [PSK[4]], onesb[:], "onesb", SQ[:, 0:Tt], "asq", Tt,
                          SQH[0][:, 0:Tt], ("sqh", 0), SQL[0][:, 0:Tt], ("sql", 0), True, True)
                P.op("act", lambda e: e.activation(out=RT[:, 0:Tt], in_=ps[4][:, 0:Tt], func=AF.Sqrt, scale=1.0 / HD,
                                                   bias=epsc[:, 0:1]), r=[PSK[4], "epsc"], w=["art"])
                P.op("dve", lambda e: e.reciprocal(out=RT[:, 0:Tt], in_=RT[:, 0:Tt]), r=["art"], w=["art"])
                if rope:
                    split2_mm(ps[5][:, 0:Tt], PS5, perm[:], "perm", QG[:, 0:Tt], "aqg", Tt,
                              QGH[:, 0:Tt], "aqgh", QGL[:, 0:Tt], "aqgl", True, True)
                    P.op("dve", lambda e: e.tensor_tensor(out=T1[:, 0:Tt], in0=QG[:, 0:Tt], in1=ropec[:, c0:c0 + Tt], op=ALU.mult),
                         r=["aqg", "ropec"], w=["at1"])
                    P.op("dve", lambda e: e.tensor_tensor(out=T2[:, 0:Tt], in0=ps[5][:, 0:Tt], in1=ropes[:, c0:c0 + Tt], op=ALU.mult),
                         r=PS5 + ["ropes"], w=["at2"])
                    P.op("dve", lambda e: e.tensor_tensor(out=T1[:, 0:Tt], in0=T1[:, 0:Tt], in1=T2[:, 0:Tt], op=ALU.add),
                         r=["at1", "at2"], w=["at1"])
                    src, sk = T1, "at1"
                else:
                    src, sk = QG, "aqg"
                P.op("dve", lambda e: e.tensor_tensor(out=dst, in0=src[:, 0:Tt], in1=RT[:, 0:Tt], op=ALU.mult),
                     r=[sk, "art"], w=[dk])

            for g in range(NKV):
                for (c0, Tt) in tiles + [ctile]:
                    raw, rk = load_raw(OFF["ka"] + g * 128, c0, Tt)
                    norm_rope(raw, rk, kg_, "kgain", c0, Tt, c0 < T, KR[:, c0:c0 + Tt], "KR")
                    raw, rk = load_raw(OFF["va"] + g * 128, c0, Tt)
                    P.op("act", lambda e, raw=raw, Tt=Tt: e.activation(out=VB[:, 0:Tt], in_=raw[:, 0:Tt], func=AF.Copy),
                         r=[rk], w=["avb"])
                    for tb in range(Tt // 128):
                        kb = c0 // 128 + tb
                        P.op("pe", lambda e, tb=tb: e.matmul(psb6[:, 0:128], VB[:, tb * 128:(tb + 1) * 128], identb[:], start=True, stop=True),
                             r=["avb", "identb"], w=["psb6"])
                        P.op("dve", lambda e, kb=kb: e.tensor_copy(out=VT[:, kb, :], in_=psb6[:, 0:128]),
                             r=["psb6"], w=["VT"])
                for hq in range(4):
                    h = g * 4 + hq
                    for qi, (c0, Tt) in enumerate(tiles):
                        raw, rk = load_raw(OFF["qa"] + h * 128, c0, Tt)
                        qr = QR[qi % 2]
                        norm_rope(raw, rk, qg_, "qgain", c0, Tt, True, qr[:, 0:Tt], ("QR", qi % 2))
                        nkb = S // 128

                        def smm(kb, qr=qr, qi=qi, Tt=Tt):
                            P.op("pe", lambda e: e.matmul(ps[kb % 2][:, 0:Tt], KR[:, kb * 128:(kb + 1) * 128], qr[:, 0:Tt],
                                                          start=True, stop=True),
                                 r=["KR", ("QR", qi % 2)], w=[PSK[kb % 2]])
                        smm(0)
                        for kb in range(nkb):
                            pt = PT[kb % 3]
                            P.op("act", lambda e, kb=kb, pt=pt, Tt=Tt: e.activation(out=pt[:, 0:Tt], in_=ps[kb % 2][:, 0:Tt],
                                                                                    func=AF.Exp, scale=SC),
                                 r=[PSK[kb % 2]], w=[("PT", kb % 3)])
                            if kb + 1 < nkb:
                                smm(kb + 1)
                            P.op("pe", lambda e, kb=kb, pt=pt, Tt=Tt: e.matmul(ps[2][:, 0:Tt], VT[:, kb, :], pt[:, 0:Tt],
                                                                               start=(kb == 0), stop=(kb == nkb - 1)),
                                 r=["VT", ("PT", kb % 3)], w=[PSK[2]])
                            P.op("pe", lambda e, kb=kb, pt=pt, Tt=Tt: e.matmul(ps[3][:, 0:Tt], onesb[:], pt[:, 0:Tt],
                                                                               start=(kb == 0), stop=(kb == nkb - 1)),
                                 r=["onesb", ("PT", kb % 3)], w=[PSK[3]])
                        P.op("dve", lambda e, Tt=Tt: e.reciprocal(out=RSUM[:, 0:Tt], in_=ps[3][:, 0:Tt]), r=[PSK[3]], w=["arsum"])
                        ob = OB[qi % 2]
                        P.op("dve", lambda e, ob=ob, Tt=Tt: e.tensor_tensor(out=ob[:, 0:Tt], in0=ps[2][:, 0:Tt], in1=RSUM[:, 0:Tt],
                                                                            op=ALU.mult),
                             r=[PSK[2], "arsum"], w=[("aob", qi % 2)])
                        P.dma("sp", catT[h * 128:(h + 1) * 128, c0:c0 + Tt], ob[:, 0:Tt], r=[("aob", qi % 2)],
                              w=[("catT", h, c0)])

        for st in _phase(STOPI >= _ORD.index('C')):
            P.barrier()
            lbv = sb(st, "lbv", [128, 2, HH]); oml = sb(st, "oml", [128, 2, HH])
            noml = sb(st, "noml", [128, 2, HH]); lsum = sb(st, "lsum", [128, 2, HH])
            maskf = sb(st, "maskf", [128, 128]); maskb = sb(st, "maskb", [128, 128]); rmask = sb(st, "rmask", [128, TT])
            P.dma("sp", maskf[:], k_maskf, w=["maskf"]); P.dma("sp", maskb[:], k_maskb, w=["maskb"])
            P.op("act", lambda e: e.activation(out=LB[:], in_=LB[:], func=AF.Exp), r=["LB"], w=["LB"])
            P.op("dve", lambda e: e.tensor_tensor(out=lsum[:], in0=LB[:, :, 0, :], in1=LB[:, :, 1, :], op=ALU.add), r=["LB"], w=["lsum"])
            P.op("dve", lambda e: e.tensor_tensor(out=lsum[:], in0=lsum[:], in1=LB[:, :, 2, :], op=ALU.add), r=["LB", "lsum"], w=["lsum"])
            P.op("dve", lambda e: e.reciprocal(out=lsum[:], in_=lsum[:]), r=["lsum"], w=["lsum"])
            P.op("dve", lambda e: e.tensor_tensor(out=lbv[:], in0=LB[:, :, 0, :], in1=lsum[:], op=ALU.mult), r=["LB", "lsum"], w=["lbv"])
            P.op("dve", lambda e: e.tensor_scalar(out=oml[:], in0=lbv[:], scalar1=-1.0, scalar2=1.0, op0=ALU.mult, op1=ALU.add),
                 r=["lbv"], w=["oml"])
            P.op("dve", lambda e: e.tensor_scalar(out=noml[:], in0=oml[:], scalar1=-1.0, scalar2=None, op0=ALU.mult),
                 r=["oml"], w=["noml"])
            P.op("dve", lambda e: e.memset(rmask[:], 1.0), w=["rmask"])
            P.op("dve", lambda e: e.memset(rmask[:].rearrange("p (c l) -> p c l", l=32)[:, :, 0:1], 0.0), w=["rmask"])

            OACC = sb(st, "OACC", [128, T])
            QS = sb(st, "QS", [128, T])
            VTK = sb(st, "VTK", [128, S // 128, 128], BF16)
            VCH = sb(st, "VCH", [32, S // 32, 128], BF16)
            Z = [sb(st, "hz%d" % i, [128, TT]) for i in range(2)]
            SGm = sb(st, "hsg", [128, TT]); LF = sb(st, "hlf", [128, TT]); KK = sb(st, "hk", [128, TT])
            CUM = sb(st, "hcum", [128, TT]); CC = sb(st, "hcc", [128, TT]); EQ = sb(st, "heq", [128, TT]); EK = sb(st, "hek", [128, TT])
            DEC = [sb(st, "hdec%d" % i, [128, TT // 32, 1]) for i in range(2)]
            QP = [sb(st, "hqp%d" % i, [128, TT], BF16) for i in range(2)]
            KTL = [sb(st, "hkt%d" % i, [128, TT], BF16) for i in range(2)]
            KP = sb(st, "hkp", [128, TT]); KPB = sb(st, "hkpb", [128, TT], BF16)
            KPT = [sb(st, "hkpt%d" % i, [32, TT // 32, 128], BF16) for i in range(2)]
            AM = [sb(st, "ham%d" % i, [128, 128], BF16) for i in range(2)]
            S32 = sb(st, "hS32", [128, 128]); SBF = sb(st, "hSbf", [128, 128], BF16)
            RAWH = [sb(st, "hraw%d" % i, [128, TT]) for i in range(2)]
            VBh = sb(st, "hvb", [128, TT], BF16)
            GB = sb(st, "hgb", [128, TT]); Y1 = sb(st, "hy1", [128, TT]); YB = [sb(st, "hyb%d" % i, [128, TT], BF16) for i in range(2)]
            hc = [0]

            def hload(row0, c0, Tt):
                i = hc[0] % 2
                hc[0] += 1
                P.dma("sp", RAWH[i][:, 0:Tt], projT[row0:row0 + 128, c0:c0 + Tt], r=[("projT", row0, c0)], w=[("hraw", i)])
                return RAWH[i], ("hraw", i)

            seq_tiles = [ctile] + tiles

            for h in range(HH):
                for (c0, Tt) in tiles + [ctile]:
                    if c0 < T:
                        raw, rk = hload(OFF["qb"] + h * 128, c0, Tt)
                        P.op("act", lambda e, raw=raw, c0=c0, Tt=Tt: e.activation(out=QS[:, c0:c0 + Tt], in_=raw[:, 0:Tt], func=AF.Silu),
                             r=[rk], w=["QS"])
                    raw, rk = hload(OFF["ib"] + h * 128, c0, Tt)
                    P.op("act", lambda e, raw=raw, Tt=Tt: e.activation(out=VBh[:, 0:Tt], in_=raw[:, 0:Tt], func=AF.Copy), r=[rk], w=["hvb"])
                    for tb in range(Tt // 128):
                        kb = c0 // 128 + tb
                        P.op("pe", lambda e, tb=tb: e.matmul(psb6[:, 0:128], VBh[:, tb * 128:(tb + 1) * 128], identb[:], start=True, stop=True),
                             r=["hvb", "identb"], w=["psb6"])
                        P.op("dve", lambda e, kb=kb: e.tensor_copy(out=VTK[:, kb, :], in_=psb6[:, 0:128]), r=["psb6"], w=["VTK"])
                        for cc_ in range(4):
                            P.op("pe", lambda e, tb=tb, cc_=cc_: e.matmul(psb7[0:32, cc_ * 128:(cc_ + 1) * 128],
                                VBh[:, tb * 128 + cc_ * 32:tb * 128 + (cc_ + 1) * 32], identb[:], start=True, stop=True),
                                r=["hvb", "identb"], w=["psb7"])
                        P.op("act", lambda e, kb=kb: e.activation(
                            out=VCH[:, kb * 4:(kb + 1) * 4, :],
                            in_=psb7[0:32, 0:512].rearrange("p (c d) -> p c d", d=128), func=AF.Copy),
                            r=["psb7"], w=["VCH"])
                for d in range(2):
                    zrow = (OFF["zf"] if d == 0 else OFF["zb"]) + h * 128
                    order = seq_tiles if d == 0 else [ctile] + tiles[::-1]
                    mask = maskf if d == 0 else maskb
                    mk = "maskf" if d == 0 else "maskb"
                    P.op("dve", lambda e: e.memset(S32[:], 0.0), w=["hS32"])
                    P.op("dve", lambda e: e.memset(SBF[:], 0.0), w=["hSbf"])
                    for ti, (c0, Tt) in enumerate(order):
                        lat = c0 < T
                        nch = Tt // 32
                        b2 = ti % 2
                        raw, rk = hload(zrow, c0, Tt)
                        lbA = lbv[:, d, h:h + 1]; omA = oml[:, d, h:h + 1]; nomA = noml[:, d, h:h + 1]
                        P.op("act", lambda e, raw=raw, Tt=Tt: e.activation(out=SGm[:, 0:Tt], in_=raw[:, 0:Tt], func=AF.Sigmoid), r=[rk], w=["hsg"])
                        P.op("act", lambda e, Tt=Tt, omA=omA, lbA=lbA: e.activation(out=LF[:, 0:Tt], in_=SGm[:, 0:Tt], func=AF.Ln, scale=omA, bias=lbA),
                             r=["hsg", "oml", "lbv"], w=["hlf"])
                        P.op("dve", lambda e, Tt=Tt, omA=omA, nomA=nomA: e.tensor_scalar(out=KK[:, 0:Tt], in0=SGm[:, 0:Tt], scalar1=nomA, scalar2=omA,
                                                                                         op0=ALU.mult, op1=ALU.add),
                             r=["hsg", "oml", "noml"], w=["hk"])
                        P.op("dve", lambda e, Tt=Tt: e.tensor_tensor_scan(out=CUM[:, 0:Tt], data0=rmask[:, 0:Tt], data1=LF[:, 0:Tt], initial=0.0,
                                                                          op0=ALU.mult, op1=ALU.add),
                             r=["hlf", "rmask"], w=["hcum"])
                        cum3 = CUM[:, 0:Tt].rearrange("p (c l) -> p c l", l=32)
                        tot = cum3[:, :, 31:32]
                        if d == 0:
                            ccsrc, cck = CUM, "hcum"
                        else:
                            P.op("dve", lambda e, Tt=Tt, cum3=cum3, tot=tot, nch=nch: e.tensor_tensor(
                                out=CC[:, 0:Tt].rearrange("p (c l) -> p c l", l=32), in0=tot.to_broadcast([128, nch, 32]), in1=cum3,
                                op=ALU.subtract), r=["hcum"], w=["hcc"])
                            P.op("dve", lambda e, Tt=Tt: e.tensor_tensor(out=CC[:, 0:Tt], in0=CC[:, 0:Tt], in1=LF[:, 0:Tt], op=ALU.add),
                                 r=["hcc", "hlf"], w=["hcc"])
                            ccsrc, cck = CC, "hcc"
                        dec = DEC[b2]
                        P.op("act", lambda e, dec=dec, tot=tot, nch=nch: e.activation(out=dec[:, 0:nch, :], in_=tot, func=AF.Exp),
                             r=["hcum"], w=[("hdec", b2)])
                        P.op("act", lambda e, Tt=Tt, ccsrc=ccsrc: e.activation(out=EK[:, 0:Tt], in_=ccsrc[:, 0:Tt], func=AF.Exp, scale=-1.0),
                             r=[cck], w=["hek"])
                        ktl = KTL[b2]
                        P.op("dve", lambda e, Tt=Tt: e.tensor_tensor(out=KP[:, 0:Tt], in0=KK[:, 0:Tt], in1=EK[:, 0:Tt], op=ALU.mult),
                             r=["hk", "hek"], w=["hkp"])
                        P.op("act", lambda e, Tt=Tt, ktl=ktl: e.activation(out=ktl[:, 0:Tt], in_=KP[:, 0:Tt], func=AF.Copy),
                             r=["hkp"], w=[("hkt", b2)])
                        P.op("dve", lambda e, Tt=Tt, dec=dec, nch=nch: e.tensor_tensor(
                            out=KPB[:, 0:Tt].rearrange("p (c l) -> p c l", l=32), in0=KP[:, 0:Tt].rearrange("p (c l) -> p c l", l=32),
                            in1=dec[:, 0:nch, :].to_broadcast([128, nch, 32]), op=ALU.mult),
                            r=["hkp", ("hdec", b2)], w=["hkpb"])
                        kpt = KPT[b2]
                        for cb in range(nch // 4):
                            for cc_ in range(4):
                                ch = cb * 4 + cc_
                                P.op("pe", lambda e, ch=ch, cc_=cc_: e.matmul(psb7[0:32, cc_ * 128:(cc_ + 1) * 128], KPB[:, ch * 32:(ch + 1) * 32], identb[:], start=True, stop=True),
                                    r=["hkpb", "identb"], w=["psb7"])
                            P.op("act", lambda e, cb=cb, kpt=kpt: e.activation(
                                out=kpt[:, cb * 4:(cb + 1) * 4, :],
                                in_=psb7[0:32, 0:512].rearrange("p (c d) -> p c d", d=128), func=AF.Copy),
                                r=["psb7"], w=[("hkpt", b2)])
                        if lat:
                            qp = QP[b2]
                            P.op("act", lambda e, Tt=Tt, ccsrc=ccsrc: e.activation(out=EQ[:, 0:Tt], in_=ccsrc[:, 0:Tt], func=AF.Exp),
                                 r=[cck], w=["heq"])
                            P.op("dve", lambda e, Tt=Tt, c0=c0, qp=qp: e.tensor_tensor(out=qp[:, 0:Tt], in0=QS[:, c0:c0 + Tt], in1=EQ[:, 0:Tt], op=ALU.mult),
                                 r=["QS", "heq"], w=[("hqp", b2)])
                        nblk = Tt // 128
                        blks = range(nblk) if d == 0 else range(nblk - 1, -1, -1)
                        for bi_, tb in enumerate(blks):
                            kb = c0 // 128 + tb
                            bsl = slice(tb * 128, (tb + 1) * 128)
                            if lat:
                                am = AM[bi_ % 2]
                                P.op("pe", lambda e, ktl=ktl, qp=qp, bsl=bsl: e.matmul(ps[0][:, 0:128], ktl[:, bsl], qp[:, bsl], start=True, stop=True),
                                     r=[("hkt", b2), ("hqp", b2)], w=[PSK[0]])
                                P.op("dve", lambda e, am=am, mask=mask: e.tensor_tensor(out=am[:], in0=ps[0][:, 0:128], in1=mask[:], op=ALU.mult),
                                     r=[PSK[0], mk], w=[("ham", bi_ % 2)])
                                P.op("pe", lambda e, am=am, kb=kb: e.matmul(ps[1][:, 0:128], VTK[:, kb, :], am[:], start=True, stop=False),
                                     r=["VTK", ("ham", bi_ % 2)], w=[PSK[1]])
                            chs = range(4) if d == 0 else range(3, -1, -1)
                            for ci_, cc_ in enumerate(chs):
                                ch = tb * 4 + cc_
                                gch = kb * 4 + cc_
                                if lat:
                                    P.op("pe", lambda e, qp=qp, tb=tb, cc_=cc_, ci_=ci_: e.matmul(
                                        ps[1][:, cc_ * 32:(cc_ + 1) * 32], SBF[:], qp[:, tb * 128 + cc_ * 32:tb * 128 + (cc_ + 1) * 32],
                                        start=False, stop=(ci_ == 3)),
                                        r=["hSbf", ("hqp", b2)], w=[PSK[1]])
                                P.op("pe", lambda e, kpt=kpt, ch=ch, gch=gch: e.matmul(ps[2][:, 0:128], kpt[:, ch, :], VCH[:, gch, :], start=True, stop=True),
                                     r=[("hkpt", b2), "VCH"], w=[PSK[2]])
                                P.op("dve", lambda e, dec=dec, ch=ch: e.scalar_tensor_tensor(out=S32[:], in0=S32[:], scalar=dec[:, ch, :], in1=ps[2][:, 0:128],
                                                                                         op0=ALU.mult, op1=ALU.add),
                                     r=["hS32", ("hdec", b2), PSK[2]], w=["hS32"])
                                P.op("act", lambda e: e.activation(out=SBF[:], in_=S32[:], func=AF.Copy), r=["hS32"], w=["hSbf"])
                            if lat:
                                osl = slice(c0 + tb * 128, c0 + (tb + 1) * 128)
                                if d == 0:
                                    P.op("act", lambda e, osl=osl: e.activation(out=OACC[:, osl], in_=ps[1][:, 0:128], func=AF.Copy),
                                         r=[PSK[1]], w=["OACC"])
                                else:
                                    P.op("dve", lambda e, osl=osl: e.tensor_tensor(out=OACC[:, osl], in0=OACC[:, osl], in1=ps[1][:, 0:128], op=ALU.add),
                                         r=[PSK[1], "OACC"], w=["OACC"])
                for qi, (c0, Tt) in enumerate(tiles):
                    raw, rk = hload(OFF["gb"] + h * 128, c0, Tt)
                    P.op("act", lambda e, raw=raw, Tt=Tt: e.activation(out=GB[:, 0:Tt], in_=raw[:, 0:Tt], func=AF.Silu), r=[rk], w=["hgb"])
                    P.op("act", lambda e, c0=c0, Tt=Tt: e.activation(out=Y1[:, 0:Tt], in_=OACC[:, c0:c0 + Tt], func=AF.Square), r=["OACC"], w=["hy1"])
                    split2_mm(ps[4][:, 0:Tt], [PSK[4]], onesb[:], "onesb", Y1[:, 0:Tt], "hy1", Tt,
                              SQH[0][:, 0:Tt], ("sqh", 0), SQL[0][:, 0:Tt], ("sql", 0), True, True)
                    P.op("act", lambda e, Tt=Tt: e.activation(out=Y1[:, 0:Tt], in_=ps[4][:, 0:Tt], func=AF.Sqrt, scale=1.0 / HD, bias=epsc[:, 0:1]),
                         r=[PSK[4], "epsc"], w=["hy1"])
                    P.op("dve", lambda e, Tt=Tt: e.reciprocal(out=Y1[:, 0:Tt], in_=Y1[:, 0:Tt]), r=["hy1"], w=["hy1"])
                    P.op("dve", lambda e, c0=c0, Tt=Tt: e.scalar_tensor_tensor(out=Y1[:, 0:Tt], in0=OACC[:, c0:c0 + Tt], scalar=og_[:, 0:1], in1=Y1[:, 0:Tt],
                                                                           op0=ALU.mult, op1=ALU.mult), r=["OACC", "ogain", "hy1"], w=["hy1"])
                    yb = YB[qi % 2]
                    P.op("dve", lambda e, Tt=Tt, yb=yb: e.tensor_tensor(out=yb[:, 0:Tt], in0=Y1[:, 0:Tt], in1=GB[:, 0:Tt], op=ALU.mult),
                         r=["hy1", "hgb"], w=[("hyb", qi % 2)])
                    P.dma("sp", catT[AW + h * 128:AW + (h + 1) * 128, c0:c0 + Tt], yb[:, 0:Tt], r=[("hyb", qi % 2)],
                          w=[("catT", NQ + h, c0)])

        for st in _phase(STOPI >= _ORD.index('D')):
            P.barrier()
            WB = [sb(st, "wbd%d" % i, [128, KT, 512], BF16) for i in range(NWB)]
            H = sb(st, "Hd", [128, DC, TT], BF16)
            A = sb(st, "Ad", [128, FC, TT], BF16)
            XC = [sb(st, "xcd%d" % i, [128, TT]) for i in range(4)]
            XO = [sb(st, "xod%d" % i, [128, TT]) for i in range(4)]
            SQ = [sb(st, "sqd%d" % i, [128, TT]) for i in range(2)]
            TMP = [sb(st, "tmpd%d" % i, [128, TT]) for i in range(2)]
            SG = [sb(st, "sgd%d" % i, [128, TT]) for i in range(4)]
            wci = conv_w_in[0]
            for (c0, Tt) in tiles:
                P.dma("sp", H[:, :, 0:Tt], catT[:, c0:c0 + Tt].rearrange("(c p) t -> p c t", p=128),
                      r=[("catT", hh, c0) for hh in range(DC)], w=[("H", n) for n in range(DC)])
                gemm(WB, H, lambda k: ("H", k), DC, Tt, colblocks(ab_w_out[0], 0, D),
                     resid_epilogue(c0, Tt, lambda n: RS[:, 0, 1, n, 0:1], XC, XO, SQ))
                ffn(WB, c0, Tt, 0, 1, 0, H, A, XC, XO, SQ, TMP, SG)
                ffn(WB, c0, Tt, 1, 0, 0, H, A, XC, XO, SQ, TMP, SG)
                adaln(c0, Tt, 1, 1, 0, H, XC, TMP)
                def epib(bi, j, pap, pk, c0=c0, Tt=Tt):
                    n = bi * 4 + j
                    xo = XO[n % 4]
                    P.op("act", lambda e: e.activation(out=xo[:, 0:Tt], in_=pap, func=AF.Copy), r=[pk], w=[("xo", n % 4)])
                    P.dma("sp", cbT[n * 128:(n + 1) * 128, c0:c0 + Tt], xo[:, 0:Tt], r=[("xo", n % 4)], w=[("cbT", n, c0)])
                gemm(WB, H, lambda k: ("H", k), DC, Tt, colblocks(wci, 0, D), epib)
                blocks = [[wci[:, D + j0 * 128:D + j0 * 128 + 256], wci[:, 2 * D + j0 * 128:2 * D + j0 * 128 + 256]] for j0 in range(0, DC, 2)]

                def epiu(bi, j, pap, pk, c0=c0, Tt=Tt):
                    jj = j % 2
                    if j < 2:
                        P.op("act", lambda e: e.activation(out=SG[jj][:, 0:Tt], in_=pap, func=AF.Copy), r=[pk], w=[("sg", jj)])
                    else:
                        n = bi * 2 + jj
                        xo = XO[n % 4]
                        P.op("dve", lambda e: e.tensor_tensor(out=xo[:, 0:Tt], in0=SG[jj][:, 0:Tt], in1=pap, op=ALU.mult),
                             r=[pk, ("sg", jj)], w=[("xo", n % 4)])
                        P.dma("sp", cuT[n * 128:(n + 1) * 128, c0:c0 + Tt], xo[:, 0:Tt], r=[("xo", n % 4)], w=[("cuT", n, c0)])
                gemm(WB, H, lambda k: ("H", k), DC, Tt, blocks, epiu)

            UH = [sb(st, "uh%d" % i, [128, TT + 2]) for i in range(2)]
            BG = [sb(st, "bg%d" % i, [128, TT]) for i in range(2)]
            for (c0, Tt) in tiles:
                for n in range(DC):
                    uh = UH[n % 2]
                    bg = BG[n % 2]
                    lo = max(c0 - 1, 0)
                    hi = min(c0 + Tt + 1, T)
                    if c0 == 0:
                        P.op("dve", lambda e, uh=uh: e.memset(uh[:, 0:1], 0.0), w=[("uh", n % 2)])
                    if c0 + Tt == T:
                        P.op("dve", lambda e, uh=uh, Tt=Tt: e.memset(uh[:, Tt + 1:Tt + 2], 0.0), w=[("uh", n % 2)])
                    P.dma("sp", uh[:, lo - (c0 - 1):hi - (c0 - 1)], cuT[n * 128:(n + 1) * 128, lo:hi],
                          r=[("cuT", n, cc0) for (cc0, _t) in tiles], w=[("uh", n % 2)])
                    P.dma("sp", bg[:, 0:Tt], cbT[n * 128:(n + 1) * 128, c0:c0 + Tt], r=[("cbT", n, c0)], w=[("bg", n % 2)])
                    t = TMP[n % 2]
                    P.op("dve", lambda e, uh=uh, t=t, n=n, Tt=Tt: e.tensor_scalar(out=t[:, 0:Tt], in0=uh[:, 0:Tt], scalar1=cwT[:, 0, n:n + 1], scalar2=None,
                                                                               op0=ALU.mult), r=[("uh", n % 2), "cwT"], w=[("tmp", n % 2)])
                    P.op("dve", lambda e, uh=uh, t=t, n=n, Tt=Tt: e.scalar_tensor_tensor(out=t[:, 0:Tt], in0=uh[:, 1:Tt + 1], scalar=cwT[:, 1, n:n + 1],
                                                                                      in1=t[:, 0:Tt], op0=ALU.mult, op1=ALU.add),
                         r=[("uh", n % 2), "cwT", ("tmp", n % 2)], w=[("tmp", n % 2)])
                    P.op("dve", lambda e, uh=uh, t=t, n=n, Tt=Tt: e.scalar_tensor_tensor(out=t[:, 0:Tt], in0=uh[:, 2:Tt + 2], scalar=cwT[:, 2, n:n + 1],
                                                                                      in1=t[:, 0:Tt], op0=ALU.mult, op1=ALU.add),
                         r=[("uh", n % 2), "cwT", ("tmp", n % 2)], w=[("tmp", n % 2)])
                    P.op("dve", lambda e, bg=bg, t=t, n=n, Tt=Tt: e.tensor_tensor(out=H[:, n, 0:Tt], in0=t[:, 0:Tt], in1=bg[:, 0:Tt], op=ALU.mult),
                         r=[("tmp", n % 2), ("bg", n % 2)], w=[("H", n)])
                gemm(WB, H, lambda k: ("H", k), DC, Tt, colblocks(conv_w_out[0], 0, D),
                     resid_epilogue(c0, Tt, lambda n: RS[:, 1, 1, n, 0:1], XC, XO, SQ))
                ffn(WB, c0, Tt, 1, 1, 0, H, A, XC, XO, SQ, TMP, SG)

        for st in _phase(STOPI >= _ORD.index('D')):
            P.barrier()
            XCF = [sb(st, "xcf%d" % i, [128, TT]) for i in range(2)]
            FR1 = sb(st, "fr1", [128, TT])
            FS = [sb(st, "fs%d" % i, [128, TT], BF16) for i in range(3)]
            OUTB = [sb(st, "outb%d" % i, [128, 4, 512]) for i in range(2)]
            gi = 0
            for (c0, Tt) in tiles:
                nb = Tt // 128
                for n in range(DC):
                    sl = n % 4
                    if sl == 0:
                        oset = gi % 2
                        gi += 1
                    xc = XCF[n % 2]
                    xk = ("xcf", n % 2)
                    bank, bkey = TBK[n % 2]
                    P.dma("sp", xc[:, 0:Tt], xT[n * 128:(n + 1) * 128, c0:c0 + Tt], r=[("xT", n, c0)], w=[xk])
                    P.op("act", lambda e: e.activation(out=FS[0][:, 0:Tt], in_=xc[:, 0:Tt], func=AF.Copy), r=[xk], w=["fs0"])
                    P.op("dve", lambda e: e.tensor_tensor(out=FR1[:, 0:Tt], in0=xc[:, 0:Tt], in1=FS[0][:, 0:Tt], op=ALU.subtract),
                         r=[xk, "fs0"], w=["fr1"])
                    P.op("act", lambda e: e.activation(out=FS[1][:, 0:Tt], in_=FR1[:, 0:Tt], func=AF.Copy), r=["fr1"], w=["fs1"])
                    P.op("dve", lambda e: e.tensor_tensor(out=FS[2][:, 0:Tt], in0=FR1[:, 0:Tt], in1=FS[1][:, 0:Tt], op=ALU.subtract),
                         r=["fr1", "fs1"], w=["fs2"])
                    for tb in range(nb):
                        for q3 in range(3):
                            P.op("pe", lambda e: e.matmul(bank[:, tb * 128:(tb + 1) * 128], FS[q3][:, tb * 128:(tb + 1) * 128], identb[:],
                                                          start=(q3 == 0), stop=(q3 == 2)),
                                 r=["fs%d" % q3, "identb"], w=[bkey])
                    ob = OUTB[oset]
                    obk = ("outb", oset)
                    P.op("act", lambda e: e.activation(out=ob[:, 0:nb, sl * 128:(sl + 1) * 128],
                                                       in_=bank[:, 0:nb * 128].rearrange("p (b f) -> p b f", f=128), func=AF.Copy),
                         r=[bkey], w=[obk])
                    if sl == 3:
                        for tb in range(nb):
                            P.dma("sp", out[c0 + tb * 128:c0 + (tb + 1) * 128, (n - 3) * 128:(n + 1) * 128], ob[:, tb, :], r=[obk])

        P.emit()
    return nc


def host_consts(cfg):
    T, GW = cfg["T"], cfg["GRID_W"]
    t = np.arange(T)
    pos = np.stack([t // GW, t % GW]).astype(np.float32)
    inv = (10000.0 ** (-np.arange(0, 64, 2, dtype=np.float32) / 64.0)).astype(np.float32)
    p = np.arange(128)
    a = p // 64
    half = (p % 64) // 32
    i = p % 32
    ang = pos[a, :] * inv[i][:, None]
    ropec = np.cos(ang).astype(np.float32)
    sgn = np.where(half == 0, -1.0, 1.0).astype(np.float32)[:, None]
    ropes = (np.sin(ang) * sgn).astype(np.float32)
    pair = np.where(half == 0, p + 32, p - 32)
    perm = np.zeros((128, 128), np.float32)
    perm[pair, p] = 1.0
    ident = np.eye(128, dtype=np.float32)
    s = p[:, None]
    tt = p[None, :]
    same = (s // 32) == (tt // 32)
    maskf = (same & (s <= tt)).astype(np.float32)
    maskb = (same & (s >= tt)).astype(np.float32)
    return dict(k_ropec=ropec, k_ropes=ropes, k_perm=perm.astype(ml_dtypes.bfloat16), k_ident=ident,
                k_identb=ident.astype(ml_dtypes.bfloat16), k_maskf=maskf, k_maskb=maskb)


_NC_CACHE = {}


def run(cfg, inputs, dbg=()):
    key = (tuple(sorted(cfg.items())), tuple(dbg))
    if key not in _NC_CACHE:
        _NC_CACHE[key] = build_nc(cfg, dbg)
    nc = _NC_CACHE[key]
    B = inputs["x"].shape[0]
    consts = host_consts(cfg)
    shared = {k: np.ascontiguousarray(v) for k, v in inputs.items() if k not in ("x", "c", "ctx", "c_ctx")}
    shared.update(consts)
    in_maps = []
    for b in range(B):
        m = dict(shared)
        m["x"] = np.ascontiguousarray(inputs["x"][b])
        m["ctx"] = np.ascontiguousarray(inputs["ctx"][b])
        m["cvec"] = np.ascontiguousarray(np.stack([inputs["c"][b], inputs["c_ctx"]]))
        in_maps.append(m)
    res = run_bass_kernel_spmd(nc, in_maps, core_ids=list(range(B)))
    return res


def kernel(**inputs):
    inputs = {k: np.asarray(v) for k, v in inputs.items()}
    res = run(CFG, inputs)
    return np.stack([r["out"] for r in res.results]).astype(np.float32)
```
